# Optimizing a Trainium2 kernel written in Bass

```python
import jax, jax.numpy as jnp
from jax import lax
import numpy as np

D_MODEL = 2048
BATCH = 2
SEQ = 4096
DEPTH = 1
DEC_BATCH = 8
DEC_SEQ = 4
PAST_LEN = 16384
PAGE_SIZE = 128

ATTN_WIDTH = D_MODEL // 2
HEAD_DIM = 64
N_HEADS = ATTN_WIDTH // HEAD_DIM
DILATED_PATTERNS = ((128, 1), (512, 4), (2048, 16))
MAX_WINDOW = max(w for w, _ in DILATED_PATTERNS)
POOL_WIDTH = D_MODEL - ATTN_WIDTH
POOL_WINDOWS = (2, 4, 8, 16)
N_POOL_GROUPS = len(POOL_WINDOWS)
POOL_GROUP_WIDTH = POOL_WIDTH // N_POOL_GROUPS
POOL_HIST = max(POOL_WINDOWS) - 1
MIX_WIDTH = ATTN_WIDTH + POOL_WIDTH
IN_WIDTH = 4 * ATTN_WIDTH + 2 * POOL_WIDTH
BLOCK = 128
EPS = 1e-6
ATTN_SCALE = HEAD_DIM ** -0.5

kernel_name = "hymba_dilated_window_pool_hybrid_step"


def rms_norm(x, g):
    xf = x.astype(jnp.float32)
    y = xf * lax.rsqrt(jnp.mean(xf * xf, axis=-1, keepdims=True) + EPS)
    return (y * g.astype(jnp.float32)).astype(x.dtype)


def project(x, g, w_in):
    b, s, _ = x.shape
    z = jnp.einsum('bsd,de->bse', rms_norm(x, g), w_in)
    cuts = [ATTN_WIDTH, 2 * ATTN_WIDTH, 3 * ATTN_WIDTH, 4 * ATTN_WIDTH, 4 * ATTN_WIDTH + POOL_WIDTH]
    q, k, v, ga, u, gp = jnp.split(z, cuts, axis=-1)
    hs = (b, s, N_HEADS, HEAD_DIM)
    return q.reshape(hs), k.reshape(hs), v.reshape(hs), ga, u, gp


def band_attention(q, k, v, window_steps):
    assert window_steps <= BLOCK
    n, m, h, dh = q.shape
    nb = m // BLOCK
    qb = q.reshape(n, nb, BLOCK, h, dh)
    kb = k.reshape(n, nb, BLOCK, h, dh)
    vb = v.reshape(n, nb, BLOCK, h, dh)
    pad = ((0, 0), (1, 0), (0, 0), (0, 0), (0, 0))
    kcat = jnp.concatenate([jnp.pad(kb, pad)[:, :-1], kb], axis=2)
    vcat = jnp.concatenate([jnp.pad(vb, pad)[:, :-1], vb], axis=2)
    s = jnp.einsum('nbqhd,nbkhd->nbhqk', qb, kcat,
                   preferred_element_type=jnp.float32) * ATTN_SCALE
    qi = jnp.arange(BLOCK)[:, None]
    ki = jnp.arange(2 * BLOCK)[None, :] - BLOCK
    dist = qi - ki
    band = (dist >= 0) & (dist <= window_steps)
    blk_start = jnp.arange(nb) * BLOCK
    valid = band[None] & ((blk_start[:, None, None] + ki[None]) >= 0)
    s = jnp.where(valid[None, :, None], s, -jnp.inf)
    lse = jax.nn.logsumexp(s, axis=-1)
    p = jnp.exp(s - lse[..., None])
    o = jnp.einsum('nbhqk,nbkhd->nbqhd', p, vcat.astype(jnp.float32))
    o = o.reshape(n, m, h, dh)
    lse = jnp.transpose(lse, (0, 1, 3, 2)).reshape(n, m, h)
    return o, lse


def combine_patterns(outs, lses):
    w = jax.nn.softmax(jnp.stack(lses, axis=0), axis=0)
    return jnp.einsum('pbsh,pbshd->bshd', w, jnp.stack(outs, axis=0))


def dilated_prompt(q, k, v):
    b, s, h, dh = q.shape
    outs, lses = [], []
    for window, d in DILATED_PATTERNS:
        span = d * BLOCK
        sp = -(-s // span) * span
        mlen = sp // d
        padw = ((0, 0), (0, sp - s), (0, 0), (0, 0))

        def to_residue(t):
            t = jnp.pad(t, padw).reshape(b, mlen, d, h, dh)
            return jnp.swapaxes(t, 1, 2).reshape(b * d, mlen, h, dh)

        o, l = band_attention(to_residue(q), to_residue(k), to_residue(v), window // d)
        o = jnp.swapaxes(o.reshape(b, d, mlen, h, dh), 1, 2).reshape(b, sp, h, dh)[:, :s]
        l = jnp.swapaxes(l.reshape(b, d, mlen, h), 1, 2).reshape(b, sp, h)[:, :s]
        outs.append(o)
        lses.append(l)
    return combine_patterns(outs, lses).astype(q.dtype)


def dilated_sample(q, k_all, v_all, buf_len):
    t = q.shape[1]
    i = jnp.arange(t)
    outs, lses = [], []
    for window, d in DILATED_PATTERNS:
        j = jnp.arange(window // d + 1)
        idx = buf_len + i[:, None] - j[None, :] * d
        valid = idx >= 0
        idx_c = jnp.clip(idx, 0)
        kg = jnp.take(k_all, idx_c, axis=1)
        vg = jnp.take(v_all, idx_c, axis=1)
        s = jnp.einsum('bthd,btjhd->bhtj', q, kg,
                       preferred_element_type=jnp.float32) * ATTN_SCALE
        s = jnp.where(valid[None, None], s, -jnp.inf)
        l = jax.nn.logsumexp(s, axis=-1)
        p = jnp.exp(s - l[..., None])
        o = jnp.einsum('bhtj,btjhd->bthd', p, vg.astype(jnp.float32))
        outs.append(o)
        lses.append(jnp.swapaxes(l, 1, 2))
    return combine_patterns(outs, lses).astype(q.dtype)


def pool_mix(u_ext, pos, w_pool, pool_scale):
    b, le, c = u_ext.shape
    L = le - POOL_HIST
    uf = u_ext.astype(jnp.float32)
    csum = jnp.concatenate([jnp.zeros((b, 1, c), jnp.float32), jnp.cumsum(uf, axis=1)], axis=1)
    cur = uf[:, POOL_HIST:]
    groups = []
    for g, w in enumerate(POOL_WINDOWS):
        sl = slice(g * POOL_GROUP_WIDTH, (g + 1) * POOL_GROUP_WIDTH)
        win_sum = (csum[:, POOL_HIST + 1:POOL_HIST + 1 + L, sl]
                   - csum[:, POOL_HIST + 1 - w:POOL_HIST + 1 - w + L, sl])
        cnt = jnp.minimum(pos + 1, w).astype(jnp.float32)[None, :, None]
        groups.append(win_sum / cnt - cur[..., sl])
    pooled = jnp.stack(groups, axis=2)
    y = jnp.einsum('blgc,gcd->blgd', pooled, w_pool.astype(jnp.float32)).reshape(b, L, c)
    return (y * pool_scale.astype(jnp.float32)).astype(u_ext.dtype)


def merge(x, attn_o, ga, pool_o, gp, w_out):
    b, s, _ = x.shape
    mix = jnp.concatenate([attn_o.reshape(b, s, ATTN_WIDTH) * jax.nn.silu(ga),
                           pool_o * jax.nn.silu(gp)], axis=-1)
    return x + jnp.einsum('bse,ed->bsd', mix, w_out)


def setup_inputs(seed: int = 0) -> dict:
    key = jax.random.key(seed)
    ks = jax.random.split(key, 12)
    buf = min(MAX_WINDOW, PAST_LEN)
    f32 = jnp.float32
    x_prompt = jax.random.normal(ks[0], (BATCH, SEQ, D_MODEL), f32)
    x_sample = jax.random.normal(ks[1], (DEC_BATCH, DEC_SEQ, D_MODEL), f32)
    cache_k = jax.random.normal(ks[2], (DEPTH, DEC_BATCH, buf, N_HEADS, HEAD_DIM), f32)
    cache_v = jax.random.normal(ks[3], (DEPTH, DEC_BATCH, buf, N_HEADS, HEAD_DIM), f32)
    state_pool = jax.random.normal(ks[4], (DEPTH, DEC_BATCH, POOL_HIST, POOL_WIDTH), f32)
    norm_g = 1.0 + 0.02 * jax.random.normal(ks[5], (DEPTH, D_MODEL), f32)
    w_in = jax.random.normal(ks[6], (DEPTH, D_MODEL, IN_WIDTH), f32) * D_MODEL ** -0.5
    w_pool = jax.random.normal(ks[7], (DEPTH, N_POOL_GROUPS, POOL_GROUP_WIDTH, POOL_GROUP_WIDTH), f32) * POOL_GROUP_WIDTH ** -0.5
    pool_scale = 1.0 + 0.02 * jax.random.normal(ks[8], (DEPTH, POOL_WIDTH), f32)
    w_out = jax.random.normal(ks[9], (DEPTH, MIX_WIDTH, D_MODEL), f32) * MIX_WIDTH ** -0.5
    final_norm_g = 1.0 + 0.02 * jax.random.normal(ks[10], (D_MODEL,), f32)
    return {"x_prompt": x_prompt, "x_sample": x_sample, "cache_k": cache_k, "cache_v": cache_v,
            "state_pool": state_pool, "norm_g": norm_g, "w_in": w_in, "w_pool": w_pool,
            "pool_scale": pool_scale, "w_out": w_out, "final_norm_g": final_norm_g}


def reference(x_prompt, x_sample, cache_k, cache_v, state_pool, norm_g, w_in, w_pool,
              pool_scale, w_out, final_norm_g):
    hp, hs = x_prompt, x_sample
    s = hp.shape[1]
    t = hs.shape[1]
    pos_p = jnp.arange(s)
    pos_s = PAST_LEN + jnp.arange(t)
    buf = cache_k.shape[2]
    keep = min(MAX_WINDOW, s)
    kp, vp, pp, ksn, vsn, psn = [], [], [], [], [], []
    for l in range(DEPTH):
        q, k, v, ga, u, gp = project(hp, norm_g[l], w_in[l])
        a = dilated_prompt(q, k, v)
        u_ext = jnp.pad(u, ((0, 0), (POOL_HIST, 0), (0, 0)))
        po = pool_mix(u_ext, pos_p, w_pool[l], pool_scale[l])
        hp = merge(hp, a, ga, po, gp, w_out[l])
        kp.append(k[:, s - keep:])
        vp.append(v[:, s - keep:])
        pp.append(u_ext[:, -POOL_HIST:])
        q, k, v, ga, u, gp = project(hs, norm_g[l], w_in[l])
        k_all = jnp.concatenate([cache_k[l].astype(k.dtype), k], axis=1)
        v_all = jnp.concatenate([cache_v[l].astype(v.dtype), v], axis=1)
        a = dilated_sample(q, k_all, v_all, buf)
        u_ext = jnp.concatenate([state_pool[l].astype(u.dtype), u], axis=1)
        po = pool_mix(u_ext, pos_s, w_pool[l], pool_scale[l])
        hs = merge(hs, a, ga, po, gp, w_out[l])
        ksn.append(k_all[:, -buf:])
        vsn.append(v_all[:, -buf:])
        psn.append(u_ext[:, -POOL_HIST:])
    y_prompt = rms_norm(hp, final_norm_g)
    y_sample = rms_norm(hs, final_norm_g)
    return (y_prompt, y_sample, jnp.stack(kp), jnp.stack(vp), jnp.stack(pp),
            jnp.stack(ksn), jnp.stack(vsn), jnp.stack(psn))
```

```python
import numpy as np
import concourse.bass as bass
import concourse.mybir as mybir
from concourse.bass_utils import run_bass_kernel_spmd

F32 = mybir.dt.float32
BF16 = mybir.dt.bfloat16
AF = mybir.ActivationFunctionType
ALU = mybir.AluOpType
AX = mybir.AxisListType

D = 2048
NOWN = 1024
NHALO = 2048
NT = NOWN + NHALO
EPS = 1e-6
SQD = float(np.sqrt(2048.0))
ARENA_BYTES = 206 * 1024
PSUM_BYTES = 16 * 1024
GR = 64

C_MASK = 0
C_IDENT = 768
C_SEL = 896
C_MASK1 = 912
C_MNEW = 916
C_INVC = 920
C_PM = 984
NCONST = 1000


class Op:
    __slots__ = ("eng", "fn", "waits", "is_dma", "dsem", "dtarget", "milestone", "mcount", "raw", "other", "dent", "dmaw")

    def __init__(self, eng, fn, is_dma=False):
        self.eng = eng
        self.fn = fn
        self.is_dma = is_dma
        self.dsem = None
        self.dtarget = 0
        self.milestone = False
        self.mcount = 0
        self.raw = set()
        self.other = set()
        self.dent = None
        self.dmaw = {}


class KB:
    ENGS = ("pe", "act", "dve", "pool", "sp")

    def __init__(self, nc):
        self.nc = nc
        self.ops = []
        self.state = {}
        self.dsems = {}
        self.esem = {e: nc.alloc_semaphore("sem_" + e) for e in ("pe", "act", "dve", "pool")}

    def _ivals(self, ap):
        name = ap.tensor.name
        sz = mybir.dt.size(ap.dtype)
        dims = list(ap.ap)
        off = ap.offset
        if name == "arena":
            rowlen = ARENA_BYTES // sz
            off = off % rowlen
            dims = dims[1:]
        elif name == "psum":
            rowlen = PSUM_BYTES // sz
            off = off % rowlen
            dims = dims[1:]
        dims = [(abs(s), c) for (s, c) in dims if c > 1 and s != 0]
        dims.sort(key=lambda sc: -sc[0])
        starts = [off]
        k = 0
        prod = 1
        while k < len(dims) - 1 and prod * dims[k][1] <= 64:
            s, c = dims[k]
            starts = [st + i * s for st in starts for i in range(c)]
            prod *= c
            k += 1
        span = sum(s * (c - 1) for s, c in dims[k:]) + 1
        return name, [(st * sz, (st + span) * sz) for st in starts]

    def _grans(self, ap):
        name, ivs = self._ivals(ap)
        st = self.state.setdefault(name, {})
        out = []
        seen = set()
        gr = 2048 if name == "psum" else GR
        for lo, hi in ivs:
            for g in range(lo // gr, (hi - 1) // gr + 1):
                if g not in seen:
                    seen.add(g)
                    out.append(g)
        return st, out

    def _add(self, op, reads, writes):
        writes = list(writes) + [a for a in reads if a.tensor.name == "psum"]
        reads = [a for a in reads if a.tensor.name != "psum"]
        rg = [self._grans(a) for a in reads]
        wg = [self._grans(a) for a in writes]
        for st, gs in rg:
            for g in gs:
                rec = st.get(g)
                if rec is not None and rec[0] is not None:
                    op.raw.add(rec[0])
        for st, gs in wg:
            for g in gs:
                rec = st.get(g)
                if rec is not None:
                    if rec[0] is not None:
                        op.other.add(rec[0])
                    op.other.update(rec[1].values())
                    op.other.update(rec[2])
        for d in list(op.raw) + list(op.other):
            if d.is_dma:
                op.dmaw[id(d.dsem)] = (d.dsem, d.dent[1])
        for st, gs in rg:
            for g in gs:
                rec = st.get(g)
                if rec is None:
                    rec = [None, {}, []]
                    st[g] = rec
                if op.is_dma:
                    rec[2].append(op)
                else:
                    rec[1][op.eng] = op
        for st, gs in wg:
            for g in gs:
                st[g] = [op, {}, []]
        self.ops.append(op)
        return op

    def op(self, eng, fn, reads=(), writes=()):
        return self._add(Op(eng, fn), list(reads), list(writes))

    def dma(self, queue, out, in_, slot, **kw):
        o = Op(queue, None, is_dma=True)
        if slot not in self.dsems:
            self.dsems[slot] = [self.nc.alloc_semaphore("d_" + slot), 0]
        ent = self.dsems[slot]
        o.dsem = ent[0]
        o.dent = ent
        o.fn = lambda e: e.dma_start(out=out, in_=in_, **kw)
        self._add(o, [in_], [out])
        ent[1] += 16
        return o

    def finish(self):
        nc = self.nc
        for op in self.ops:
            deps = set()
            for d in op.raw:
                if d.is_dma or op.is_dma:
                    deps.add(d)
                elif d.eng == "pe" and op.eng == "pe":
                    continue
                else:
                    deps.add(d)
            for d in op.other:
                if d.is_dma or op.is_dma:
                    deps.add(d)
                elif d.eng == op.eng:
                    continue
                else:
                    deps.add(d)
            op.waits = deps
            for d in deps:
                if not d.is_dma:
                    d.milestone = True
        cnt = {e: 0 for e in self.ENGS}
        for op in self.ops:
            if not op.is_dma and op.milestone:
                cnt[op.eng] += 1
                op.mcount = cnt[op.eng]
        streams = {e: [] for e in self.ENGS}
        for op in self.ops:
            streams[op.eng].append(op)
        finals = [(ent[0], ent[1]) for ent in self.dsems.values()]

        def emit(ename, e):
            waited = {}
            for op in streams[ename]:
                need = {}
                for d in op.waits:
                    if d.is_dma:
                        key = id(d.dsem)
                        sem, val = op.dmaw[key]
                    else:
                        sem = self.esem[d.eng]
                        key, val = id(sem), d.mcount
                    if need.get(key, (None, 0))[1] < val:
                        need[key] = (sem, val)
                for key, (sem, val) in need.items():
                    if waited.get(key, 0) < val:
                        e.wait_ge(sem, val)
                        waited[key] = val
                ins = op.fn(e)
                if op.is_dma:
                    ins.then_inc(op.dsem, 16)
                elif op.milestone:
                    ins.then_inc(self.esem[ename], 1)
            if ename == "sp":
                for sem, val in finals:
                    e.wait_ge(sem, val)

        with nc.Block() as block:
            @block.tensor
            def _(e):
                emit("pe", e)

            @block.scalar
            def _(e):
                emit("act", e)

            @block.vector
            def _(e):
                emit("dve", e)

            @block.gpsimd
            def _(e):
                emit("pool", e)

            @block.sync
            def _(e):
                emit("sp", e)


def build_nc(upto="E", dbg=False):
    nc = bass.Bass("TRN2", target_bir_lowering=False)
    kb = KB(nc)
    order = "0ABCDSE"

    def want(stage):
        return order.index(stage) <= order.index(upto)

    def din(name, shape, dt=F32):
        return nc.dram_tensor(name, list(shape), dt, kind="ExternalInput").ap()

    def dout(name, shape, dt=F32):
        return nc.dram_tensor(name, list(shape), dt, kind="ExternalOutput").ap()

    def dscr(name, shape, dt):
        if dbg:
            return nc.dram_tensor(name, list(shape), dt, kind="ExternalOutput").ap()
        return nc.dram_tensor(name, list(shape), dt).ap()

    xall = din("xall", [NT, D])
    xs = din("xs", [4, D])
    ck = din("ck", [2048, 1024])
    cv = din("cv", [2048, 1024])
    spool = din("spool", [15, 1024])
    ng = din("ng", [1, D])
    fg = din("fg", [1, D])
    win = din("win", [D, 6144])
    wpool = din("wpool", [4, 256, 256])
    psfm = din("psfm", [128, 8])
    psrow = din("psrow", [1, 1024])
    wout = din("wout", [D, D])
    consts = din("consts", [128, NCONST])

    y_own = dout("y_own", [NOWN, D])
    k_own = dout("k_own", [NOWN, 1024])
    v_own = dout("v_own", [NOWN, 1024])
    pool_p = dout("pool_p", [16, 1024])
    y_s = dout("y_s", [4, D])
    k_samp = dout("k_samp", [2048, 1024])
    v_samp = dout("v_samp", [2048, 1024])
    pool_s = dout("pool_s", [15, 1024])

    QT = dscr("QT", [8, 128, NOWN], BF16)
    KTs = dscr("KTs", [8, 128, NT], BF16)
    VTs = dscr("VTs", [8, 128, NT], BF16)
    G2A = dscr("G2A", [8, 128, NOWN], BF16)
    UT = dscr("UT", [8, 128, 1040], F32)
    G2P = dscr("G2P", [8, 128, NOWN], BF16)
    MIXT = dscr("MIXT", [16, 128, NOWN], BF16)
    ZS = dscr("ZS", [4, 6144], F32)

    arena = nc.alloc_sbuf_tensor("arena", [128, ARENA_BYTES // 2], BF16)
    psum = nc.alloc_psum_tensor("psum", [128, PSUM_BYTES // 4], F32)

    def A(off, nbytes, dt=BF16):
        assert off % 4 == 0 and off + nbytes <= ARENA_BYTES, (off, nbytes)
        v = arena[:, off // 2:(off + nbytes) // 2]
        if dt != BF16:
            v = v.bitcast(dt)
        return v

    def PS(bank, dt=F32):
        v = psum[:, bank * 512:(bank + 1) * 512]
        if dt != F32:
            v = v.bitcast(dt)
        return v

    class Bump:
        def __init__(self, lo, hi):
            self.p = lo
            self.hi = hi

        def __call__(self, nbytes, dt=BF16):
            nb = (nbytes + 63) // 64 * 64
            v = A(self.p, nbytes, dt)
            self.p += nb
            assert self.p <= self.hi, (self.p, self.hi)
            return v

    KBY = 1024
    R0 = 0
    R1 = 96 * KBY
    R2 = R1 + 32 * KBY
    R3 = R2 + 32 * KBY

    b3 = Bump(R3, ARENA_BYTES)
    cst = b3(NCONST * 4, F32)
    MB = b3(3 * 2 * 2 * 128 * 2).rearrange("p (v j h c) -> p v j h c", v=3, j=2, h=2)
    ident = b3(128 * 2)
    ones = b3(128 * 2)
    ss = b3(32 * 4, F32)
    rs = b3(32 * 4, F32)
    ss2 = b3(16 * 4, F32)
    rs2 = b3(16 * 4, F32)
    psf = b3(8 * 4, F32)
    psh = b3(8 * 4, F32)
    wpb = b3(4 * 2 * 256 * 2).rearrange("p (g c d) -> p g c d", g=4, c=2)
    xnTs = b3(16 * 4 * 2).rearrange("p (c t) -> p c t", c=16)
    mixTs = b3(16 * 4 * 2).rearrange("p (c t) -> p c t", c=16)
    zst = b3(1024 * 4, F32)
    ppst = b3(1024 * 4, F32)
    stg = [b3(6 * KBY), b3(6 * KBY)]
    thb = [b3(512 * 4, F32), b3(512 * 4, F32)]
    xs_sb = b3(D * 4, F32)
    wslot = []
    b1 = Bump(R1, R2)
    ws32 = [b1(16 * 128 * 4, F32).rearrange("p (c n) -> p c n", c=16) for _ in range(2)]
    for i in range(4):
        wslot.append(b1(16 * 128 * 2).rearrange("p (c n) -> p c n", c=16))
    b2 = Bump(R2, R3)
    xt = [b2(D * 4, F32), b2(D * 4, F32)]
    gB = b2(D * 4, F32)
    xn2 = [b2(D * 2), b2(D * 2)]
    xn = xn2[0]
    junk = ppst.bitcast(BF16)

    xnT = A(R0, 16 * NT * 2).rearrange("p (c t) -> p c t", c=16)

    V = kb.op

    kb.dma("sp", cst, consts, "cst")
    cm4 = cst[:, C_MASK:C_MASK + 768].rearrange("p (v j c) -> p v j c", v=3, j=2)
    for h in range(2):
        V("dve", lambda e, h=h: e.tensor_scalar(out=MB[:, :, :, h, :], in0=cm4, scalar1=-1.0, scalar2=30000.0,
                                                op0=ALU.add, op1=ALU.mult),
          [cst], [MB[:, :, :, h, :]])
    V("dve", lambda e: e.tensor_copy(out=ident, in_=cst[:, C_IDENT:C_IDENT + 128]), [cst], [ident])
    V("dve", lambda e: e.memset(ones, 1.0), [], [ones])
    kb.dma("sp", psf, psfm, "psf")
    V("dve", lambda e: e.tensor_scalar(out=psh, in0=psf, scalar1=0.5, scalar2=None, op0=ALU.mult), [psf], [psh])
    kb.dma("pool", wpb, wpool.rearrange("g (c p) d -> p g c d", p=128), "wpb")
    kb.dma("sp", gB, ng.broadcast_to([128, D]), "gB")

    if not want("A"):
        kb.finish()
        return nc
    def norm_tile(src_ap, xtile, np_, col, ssb, rsb, gain, dst, tag, junk_=None):
        jk = junk if junk_ is None else junk_
        V("act", lambda e: e.activation(out=jk[0:np_, :], in_=xtile, func=AF.Square,
                                        accum_out=ssb[0:np_, col:col + 1]),
          [xtile], [jk[0:np_, :], ssb[0:np_, col:col + 1]])
        V("act", lambda e: e.activation(out=rsb[0:np_, col:col + 1], in_=ssb[0:np_, col:col + 1], func=AF.Sqrt,
                                        scale=1.0 / D, bias=EPS),
          [ssb[0:np_, col:col + 1]], [rsb[0:np_, col:col + 1]])
        V("dve", lambda e: e.reciprocal(out=rsb[0:np_, col:col + 1], in_=rsb[0:np_, col:col + 1]),
          [rsb[0:np_, col:col + 1]], [rsb[0:np_, col:col + 1]])
        V("dve", lambda e: e.scalar_tensor_tensor(out=dst, in0=xtile, scalar=rsb[0:np_, col:col + 1],
                                                  in1=gain[0:np_, :], op0=ALU.mult, op1=ALU.mult),
          [xtile, rsb[0:np_, col:col + 1], gain[0:np_, :]], [dst])

    a_cnt = [0]

    xcur = {}

    xslot = {}

    def x_issue(t):
        if t in xslot:
            return
        n_ = a_cnt[0]
        a_cnt[0] += 1
        xslot[t] = n_
        kb.dma("sp", xt[n_ % 2], xall[t * 128:(t + 1) * 128, :], "xt%d" % (n_ % 2))

    def a_norm(t):
        x_issue(t)
        n_ = xslot[t]
        xtile = xt[n_ % 2]
        xcur[t] = xn2[n_ % 2]
        norm_tile(None, xtile, 128, t, ss, rs, gB, xcur[t], "a")

    def a_trans(t, bank0):
        xn = xcur[t]
        for half in range(2):
            pst = PS(bank0 + half, BF16).rearrange("p (c t) -> p c t", c=8)
            for cc in range(8):
                c = half * 8 + cc
                V("pe", lambda e, c=c, cc=cc, pst=pst, xn=xn: e.transpose(out=pst[:, cc, :], in_=xn[:, c * 128:(c + 1) * 128],
                                                                         identity=ident),
                  [xn[:, c * 128:(c + 1) * 128], ident], [pst[:, cc, :]])
            dst = xnT[:, half * 8:half * 8 + 8, t * 128:(t + 1) * 128]
            if half == 0:
                V("act", lambda e, dst=dst, pst=pst: e.activation(out=dst, in_=pst, func=AF.Copy), [pst], [dst])
            else:
                V("dve", lambda e, dst=dst, pst=pst: e.tensor_copy(out=dst, in_=pst), [pst], [dst])

    def a_tile(t, bank0):
        a_norm(t)
        a_trans(t, bank0)

    sc_order = [0, 1, 6, 7, 2, 8, 3, 9, 4, 10, 5, 11]
    chunk_order = [4 * sc + jj for sc in sc_order for jj in range(4)]
    winv = win.rearrange("(c p) n -> p c n", p=128)

    def w_dma(n_):
        j = chunk_order[n_]
        kb.dma("sp", ws32[n_ % 2], winv[:, :, j * 128:(j + 1) * 128], "ws%d" % (n_ % 2))

    def w_cast(n_):
        V("pool", lambda e, n_=n_: e.tensor_copy(out=wslot[n_ % 4], in_=ws32[n_ % 2]), [ws32[n_ % 2]], [wslot[n_ % 4]])

    x_issue(16)
    x_issue(17)
    w_dma(0)
    w_dma(1)
    w_cast(0)
    w_cast(1)
    own_order = list(range(16, 24)) + [15]
    a_norm(own_order[0])
    for k_, t in enumerate(own_order):
        if k_ + 1 < len(own_order):
            a_norm(own_order[k_ + 1])
            if k_ + 2 < len(own_order):
                x_issue(own_order[k_ + 2])
        a_trans(t, 2 * (k_ % 2))
    w_dma(2)
    kb.dma("sp", xs_sb[0:4, :], xs, "xs")
    xns = xn[0:4, :]
    norm_tile(None, xs_sb[0:4, :], 4, 30, ss, rs, gB, xns, "s")
    psts = PS(0, BF16)[:, 0:64].rearrange("p (c t) -> p c t", c=16)
    for c in range(16):
        V("pe", lambda e, c=c: e.transpose(out=psts[:, c, :], in_=xn[0:4, c * 128:(c + 1) * 128],
                                           identity=ident[0:4, 0:4]),
          [xn[0:4, c * 128:(c + 1) * 128], ident[0:4, 0:4]], [psts[:, c, :]])
    V("dve", lambda e: e.tensor_copy(out=xnTs, in_=psts), [psts], [xnTs])

    if not want("B"):
        kb.finish()
        return nc
    OWN_G = [(NHALO, 512), (NHALO + 512, 512)]
    ALL_G = [(i * 512, 512) for i in range(6)]
    U_G = [(NHALO - 16, 16)] + OWN_G
    bankrot = [0]

    def nextbank():
        b = 4 + bankrot[0] % 4
        bankrot[0] += 1
        return b

    evac_alt = [0]
    sc_order = [0, 1, 6, 7, 2, 8, 3, 9, 4, 10, 5, 11]
    chunk_order = [4 * sc + jj for sc in sc_order for jj in range(4)]
    halo_left = list(range(14, -1, -1))
    pend = [None]
    pendq = []

    for n_, j in enumerate(chunk_order):
        w = wslot[n_ % 4]
        if len(pendq) == 2 or (pendq and not halo_left):
            a_trans(pendq.pop(0), 0)
        if halo_left:
            pendq.append(halo_left.pop())
            a_norm(pendq[-1])
            if halo_left:
                x_issue(halo_left[-1])
        if n_ + 2 < 48:
            w_cast(n_ + 2)
        if n_ + 3 < 48:
            w_dma(n_ + 3)
        kind = j // 8
        groups = {0: OWN_G, 1: ALL_G, 2: ALL_G, 3: OWN_G, 4: U_G, 5: OWN_G}[kind]
        sb = stg[n_ % 2]
        if kind == 4:
            sview = sb[:, 0:2080].bitcast(F32)
        else:
            sview = sb
        pos = 0
        for (t0, n) in groups:
            bank = nextbank()
            pb = PS(bank)[:, 0:n]
            for c in range(16):
                V("pe", lambda e, c=c, pb=pb, w=w, t0=t0, n=n: e.matmul(pb, lhsT=w[:, c, :], rhs=xnT[:, c, t0:t0 + n],
                                                                       start=(c == 0), stop=(c == 15)),
                  [w[:, c, :], xnT[:, c, t0:t0 + n]], [pb])
            dst = sview[:, pos:pos + n]
            if kind in (0, 1, 2, 4):
                if evac_alt[0] % 2 == 0:
                    V("act", lambda e, dst=dst, pb=pb: e.activation(out=dst, in_=pb, func=AF.Copy), [pb], [dst])
                else:
                    V("dve", lambda e, dst=dst, pb=pb: e.tensor_copy(out=dst, in_=pb), [pb], [dst])
                evac_alt[0] += 1
            else:
                th = thb[evac_alt[0] % 2][:, 0:n]
                evac_alt[0] += 1
                V("act", lambda e, th=th, pb=pb: e.activation(out=th, in_=pb, func=AF.Tanh, scale=0.5), [pb], [th])
                V("dve", lambda e, dst=dst, th=th, pb=pb: e.scalar_tensor_tensor(out=dst, in0=th, scalar=1.0, in1=pb,
                                                                               op0=ALU.add, op1=ALU.mult),
                  [th, pb], [dst])
            pos += n
        dest = {0: QT, 1: KTs, 2: VTs, 3: G2A, 4: UT, 5: G2P}[kind]
        kb.dma("pool", dest[j % 8], sview[:, 0:pos], "stg%d" % (n_ % 2))
        if n_ % 2 == 1:
            wa = wslot[(n_ - 1) % 4]
            assert wslot[n_ % 4].offset - wa.offset == 2048

            def w2(c, wa=wa):
                return bass.AP(wa.tensor, wa.offset + c * 128, [list(wa.ap[0]), [2048, 2], [1, 128]])

            pa = PS(2)[0:4, 0:256].rearrange("p (s n) -> p s n", s=2)
            for c in range(16):
                V("pe", lambda e, c=c, w2=w2, pa=pa: e.matmul(pa, lhsT=xnTs[:, c, :], rhs=w2(c), start=(c == 0), stop=(c == 15)),
                  [xnTs[:, c, :], w2(c)], [pa])
            c0 = ((j - 1) % 8) * 128
            zd = zst[0:4, c0:c0 + 256].rearrange("p (s n) -> p s n", s=2)
            V("dve", lambda e, zd=zd, pa=pa: e.tensor_copy(out=zd, in_=pa), [pa], [zd])
            if j % 4 == 3:
                h4 = (j % 8) // 4
                kb.dma("pool", ZS[:, (j // 4) * 512:(j // 4 + 1) * 512], zst[0:4, h4 * 512:(h4 + 1) * 512], "zst")
            if kind == 4:
                pp = PS(3)[0:16, 0:256].rearrange("p (s n) -> p s n", s=2)
                for c in range(16):
                    V("pe", lambda e, c=c, w2=w2, pp=pp: e.matmul(pp, lhsT=xnT[:, c, NT - 16:NT], rhs=w2(c),
                                                                  start=(c == 0), stop=(c == 15)),
                      [xnT[:, c, NT - 16:NT], w2(c)], [pp])
                pd = ppst[0:16, c0:c0 + 256].rearrange("p (s n) -> p s n", s=2)
                V("dve", lambda e, pd=pd, pp=pp: e.tensor_copy(out=pd, in_=pp), [pp], [pd])
                if j % 4 == 3:
                    h4 = (j % 8) // 4
                    kb.dma("pool", pool_p[:, h4 * 512:(h4 + 1) * 512], ppst[0:16, h4 * 512:(h4 + 1) * 512], "ppst")
    assert not halo_left and not pendq

    if not want("C"):
        kb.finish()
        return nc
    Wo = A(R1, 16 * D * 2).rearrange("p (e n) -> p e n", e=16)
    woutv = wout.rearrange("(e p) n -> p e n", p=128)
    for e4 in range(8):
        for hcol in range(2):
            kb.dma("pool", Wo[:, 2 * e4:2 * e4 + 2, hcol * 1024:(hcol + 1) * 1024],
                   woutv[:, 2 * e4:2 * e4 + 2, hcol * 1024:(hcol + 1) * 1024], "wo")
    for i in range(4):
        r0 = i * 511
        kb.dma("pool", k_samp[r0:r0 + 511, :], ck[4 + r0:4 + r0 + 511, :], "kcopy")
        kb.dma("pool", v_samp[r0:r0 + 511, :], cv[4 + r0:4 + r0 + 511, :], "vcopy")
    kb.dma("pool", pool_s[0:11, :], spool[4:15, :], "pscopy")
    bc = Bump(R0, R1)
    QZ = [bc(2 * NOWN * 2).rearrange("p (h t) -> p h t", h=2) for _ in range(2)]
    KTb = [bc(NT * 2) for _ in range(2)]
    VTb = [bc(NT * 2) for _ in range(2)]
    G2 = [bc(NOWN * 2) for _ in range(2)]
    Vt = bc(53 * 128 * 2).rearrange("p (s f) -> p s f", s=53)
    Eb = [bc(512 * 2) for _ in range(3)]
    Pb = [bc(512 * 2) for _ in range(3)]
    numacc = bc(2 * NOWN * 4, F32).rearrange("p (h t) -> p h t", h=2)
    denacc = bc(2 * NOWN * 4, F32).rearrange("p (h t) -> p h t", h=2)
    rbuf = bc(NOWN * 4, F32)
    r2buf = bc(NOWN * 4, F32)
    mixst = [bc(NOWN * 2) for _ in range(2)]
    kvst = [bc(1024 * 4, F32).rearrange("p (t f) -> p t f", t=8) for _ in range(2)]

    for s_ in range(2):
        V("dve", lambda e, s_=s_: e.memset(QZ[s_][64:128, 0, :], 0.0), [], [QZ[s_][64:128, 0, :]])
        V("dve", lambda e, s_=s_: e.memset(QZ[s_][0:64, 1, :], 0.0), [], [QZ[s_][0:64, 1, :]])

    def vslot(d, r, j):
        if d == 1:
            return j - 15
        if d == 4:
            return 9 + r * 3 + (j - 3)
        return 21 + r * 2 + j

    vt_list = [(1, 0, j, 128) for j in range(15, 24)]
    vt_list += [(4, r, j, 128) for r in range(4) for j in (3, 4, 5)]
    vt_list += [(16, r, j, 128 if j == 0 else 64) for r in range(16) for j in (0, 1)]
    units = [(1, 0, i, 128, 1 if i == 16 else 0) for i in range(16, 24)]
    units += [(4, r, i, 128, 1 if i == 4 else 0) for r in range(4) for i in (4, 5)]
    units += [(16, r, 1, 64, 2) for r in range(16)]
    items = [[u_] for u_ in units[:16]] + [[units[16 + 2 * k], units[17 + 2 * k]] for k in range(8)]

    def load_hp(hp):
        s_ = hp % 2
        kb.dma("sp", QZ[s_][0:64, 0, :], QT[hp, 0:64, :], "q%d" % s_)
        kb.dma("sp", QZ[s_][64:128, 1, :], QT[hp, 64:128, :], "q%d" % s_)
        kb.dma("sp", KTb[s_], KTs[hp], "k%d" % s_)
        kb.dma("sp", VTb[s_], VTs[hp], "v%d" % s_)

    def load_g2(hp):
        kb.dma("sp", G2[hp % 2], G2A[hp], "g%d" % (hp % 2))

    def combine(hp):
        s_ = hp % 2
        ms = mixst[s_]
        g2 = G2[s_]
        for h in range(2):
            lo, hi = 64 * h, 64 * h + 64
            V("act", lambda e, h=h, lo=lo, hi=hi: e.activation(out=r2buf[lo:hi, :], in_=denacc[lo:hi, h, :], func=AF.Ln),
              [denacc[lo:hi, h, :]], [r2buf[lo:hi, :]])
            V("act", lambda e, lo=lo, hi=hi: e.activation(out=rbuf[lo:hi, :], in_=r2buf[lo:hi, :], func=AF.Exp, scale=-1.0),
              [r2buf[lo:hi, :]], [rbuf[lo:hi, :]])
            V("dve", lambda e, h=h, lo=lo, hi=hi: e.scalar_tensor_tensor(out=r2buf[lo:hi, :], in0=numacc[lo:hi, h, :], scalar=0.5,
                                                                       in1=rbuf[lo:hi, :], op0=ALU.mult, op1=ALU.mult),
              [numacc[lo:hi, h, :], rbuf[lo:hi, :]], [r2buf[lo:hi, :]])
            V("dve", lambda e, lo=lo, hi=hi, ms=ms, g2=g2: e.tensor_tensor(out=ms[lo:hi, :], in0=r2buf[lo:hi, :],
                                                                         in1=g2[lo:hi, :], op=ALU.mult),
              [r2buf[lo:hi, :], g2[lo:hi, :]], [ms[lo:hi, :]])
        kb.dma("sp", MIXT[hp], ms, "mixst%d" % s_)

    load_hp(0)
    load_g2(0)
    for hp in range(8):
        s_ = hp % 2
        if hp + 1 < 8:
            load_hp(hp + 1)
        kt = KTb[s_]
        vt = VTb[s_]
        qz = QZ[s_]
        for g0 in range(0, 53, 8):
            grp = vt_list[g0:g0 + 8]
            bank = 5 + (g0 // 8) % 2
            pst = PS(bank, BF16).rearrange("p (s f) -> p s f", s=8)
            for k, (d, r, j, nk) in enumerate(grp):
                src = vt[:, d * 128 * j + r: d * 128 * j + r + d * (nk - 1) + 1: d]
                V("pe", lambda e, k=k, src=src, nk=nk, pst=pst: e.transpose(out=pst[0:nk, k, :], in_=src, identity=ident),
                  [src, ident], [pst[0:nk, k, :]])
            n = len(grp)
            V("dve", lambda e, g0=g0, n=n, pst=pst: e.tensor_copy(out=Vt[:, g0:g0 + n, :], in_=pst[:, 0:n, :]),
              [pst[:, 0:n, :]], [Vt[:, g0:g0 + n, :]])
        pst = PS(6, BF16).rearrange("p (s f) -> p s f", s=8)
        for tt in range(8):
            src = kt[:, NHALO + tt * 128: NHALO + (tt + 1) * 128]
            V("pe", lambda e, tt=tt, src=src, pst=pst: e.transpose(out=pst[:, tt, :], in_=src, identity=ident),
              [src, ident], [pst[:, tt, :]])
        V("dve", lambda e, pst=pst: e.tensor_copy(out=kvst[0], in_=pst), [pst], [kvst[0]])
        kb.dma("sp", k_own[:, hp * 128:(hp + 1) * 128].rearrange("(t p) f -> p t f", p=128), kvst[0], "kvst0")
        V("dve", lambda e: e.tensor_copy(out=kvst[1], in_=Vt[:, 1:9, :]), [Vt[:, 1:9, :]], [kvst[1]])
        kb.dma("sp", v_own[:, hp * 128:(hp + 1) * 128].rearrange("(t p) f -> p t f", p=128), kvst[1], "kvst1")

        if hp >= 1:
            combine(hp - 1)
        if hp + 1 < 8:
            load_g2(hp + 1)
        def qk(u):
            subs = items[u]
            ns = len(subs)
            nq = subs[0][3]
            sb5 = PS(u % 3)[:, 0:ns * 4 * nq].rearrange("p (s j h q) -> p s j h q", s=ns, j=2, h=2)
            for si, (d, r, i, nq_, mv) in enumerate(subs):
                q0 = d * 128 * i + r - NHALO
                qap = qz[:, :, q0: q0 + d * (nq - 1) + 1: d]
                o_all = sb5[:, si]
                mb = MB[:, mv, :, :, 0:nq]
                V("pe", lambda e, o_all=o_all, mb=mb: e.matmul(o_all, lhsT=ident, rhs=mb, start=True, stop=False),
                  [ident, mb], [o_all])
                for jj in range(2):
                    j = i - jj
                    nk = 64 if (d == 16 and j == 1) else 128
                    k0 = d * 128 * j + r
                    kap = kt[:, k0: k0 + d * (nk - 1) + 1: d]
                    o = sb5[0:nk, si, jj, :, :]
                    V("pe", lambda e, o=o, kap=kap, qap=qap, jj=jj: e.matmul(o, lhsT=kap, rhs=qap, start=False, stop=(jj == 1)),
                      [kap, qap], [o])
            sfull = PS(u % 3)[:, 0:ns * 4 * nq]
            eb = Eb[u % 3][:, 0:ns * 4 * nq]
            V("act", lambda e, eb=eb, sfull=sfull: e.activation(out=eb, in_=sfull, func=AF.Exp, scale=0.125),
              [sfull], [eb])

        def pvd(u):
            subs = items[u]
            ns = len(subs)
            nq = subs[0][3]
            bank = (3, 4, 7)[u % 3]
            od = PS(bank)[:, 0:ns * 4 * nq].rearrange("p (s x) -> p s x", s=ns)
            pb3 = Eb[u % 3][:, 0:ns * 4 * nq].rearrange("p (s j hq) -> p s j hq", s=ns, j=2)
            for si, (d, r, i, nq_, mv) in enumerate(subs):
                o_ = od[:, si, 0:2 * nq]
                dn = od[:, si, 2 * nq:4 * nq]
                for jj in range(2):
                    j = i - jj
                    nk = 64 if (d == 16 and j == 1) else 128
                    vs = Vt[0:nk, vslot(d, r, j), :]
                    pm = pb3[0:nk, si, jj, :]
                    V("pe", lambda e, jj=jj, vs=vs, pm=pm, o_=o_: e.matmul(o_, lhsT=vs, rhs=pm, start=(jj == 0), stop=(jj == 1)),
                      [vs, pm], [o_])
                for jj in range(2):
                    j = i - jj
                    nk = 64 if (d == 16 and j == 1) else 128
                    pm = pb3[0:nk, si, jj, :]
                    V("pe", lambda e, jj=jj, nk=nk, pm=pm, dn=dn: e.matmul(dn, lhsT=ones[0:nk, :], rhs=pm,
                                                                          start=(jj == 0), stop=(jj == 1)),
                      [ones[0:nk, :], pm], [dn])
            d, r, i, nq_, mv = subs[0]
            q0 = d * 128 * i + r - NHALO
            if ns == 1:
                na = numacc[:, :, q0: q0 + d * (nq - 1) + 1: d]
                da = denacc[:, :, q0: q0 + d * (nq - 1) + 1: d]
                o3 = od[:, 0, 0:2 * nq].rearrange("p (h q) -> p h q", h=2)
                d3 = od[:, 0, 2 * nq:4 * nq].rearrange("p (h q) -> p h q", h=2)
            else:
                def acc_ap(acc):
                    return bass.AP(acc.tensor, acc.offset + q0, [list(acc.ap[0]), [1, 2], [NOWN, 2], [d, nq]])
                na = acc_ap(numacc)
                da = acc_ap(denacc)
                o3 = od[:, :, 0:2 * nq].rearrange("p s (h q) -> p s h q", h=2)
                d3 = od[:, :, 2 * nq:4 * nq].rearrange("p s (h q) -> p s h q", h=2)
            if d == 1:
                V("dve", lambda e, na=na, o3=o3: e.tensor_copy(out=na, in_=o3), [o3], [na])
                V("dve", lambda e, da=da, d3=d3: e.tensor_copy(out=da, in_=d3), [d3], [da])
            else:
                V("dve", lambda e, na=na, o3=o3: e.tensor_tensor(out=na, in0=o3, in1=na, op=ALU.add), [o3, na], [na])
                V("dve", lambda e, da=da, d3=d3: e.tensor_tensor(out=da, in0=d3, in1=da, op=ALU.add), [d3, da], [da])

        for u in range(len(items)):
            qk(u)
            if u >= 2:
                pvd(u - 2)
        pvd(len(items) - 2)
        pvd(len(items) - 1)

    combine(7)

    if not want("D"):
        kb.finish()
        return nc
    bd = Bump(R0, R1)
    ub4 = [bd(2 * 1040 * 4, F32).rearrange("p (c t) -> p c t", c=2) for _ in range(4)]
    g2p4 = [bd(2 * NOWN * 2).rearrange("p (c t) -> p c t", c=2) for _ in range(4)]
    sA = bd(2 * 1040 * 4, F32).rearrange("p (c t) -> p c t", c=2)
    sB = bd(2 * 1040 * 4, F32).rearrange("p (c t) -> p c t", c=2)
    pl = bd(2 * NOWN * 2).rearrange("p (c t) -> p c t", c=2)
    t16 = bd(2 * 16 * 4, F32).rearrange("p (c t) -> p c t", c=2)
    pmst = [bd(NOWN * 2) for _ in range(2)]
    for g in range(4):
        for cc in range(2):
            kb.dma("sp", ub4[g][:, cc, :], UT[2 * g + cc], "ub%d" % g)
            kb.dma("sp", g2p4[g][:, cc, :], G2P[2 * g + cc], "g2p%d" % g)
    for g in range(4):
        wdw = 2 ** (g + 1)
        ub = ub4[g]
        g2p = g2p4[g]
        cur = ub
        bufs = [sA, sB]
        sh = 1
        for step in range(g + 1):
            nxt = bufs[step % 2]
            lo = 2 * sh - 1
            V("dve", lambda e, nxt=nxt, cur=cur, lo=lo, sh=sh: e.tensor_tensor(out=nxt[:, :, lo:1040], in0=cur[:, :, lo:1040],
                                                                                  in1=cur[:, :, lo - sh:1040 - sh], op=ALU.add),
              [cur[:, :, lo - sh:1040]], [nxt[:, :, lo:1040]])
            cur = nxt
            sh *= 2
        V("dve", lambda e, cur=cur, wdw=wdw, ub=ub: e.scalar_tensor_tensor(out=pl, in0=cur[:, :, 16:1040], scalar=1.0 / wdw,
                                                                     in1=ub[:, :, 16:1040], op0=ALU.mult, op1=ALU.subtract),
          [cur[:, :, 16:1040], ub[:, :, 16:1040]], [pl])
        for cc in range(2):
            iv = cst[:, C_INVC + g * 16:C_INVC + g * 16 + 16]
            V("dve", lambda e, cur=cur, cc=cc, iv=iv: e.tensor_tensor(out=t16[:, cc, :], in0=cur[:, cc, 16:32], in1=iv, op=ALU.mult),
              [cur[:, cc, 16:32], iv], [t16[:, cc, :]])
            V("dve", lambda e, cc=cc, ub=ub: e.tensor_tensor(out=pl[:, cc, 0:16], in0=t16[:, cc, :], in1=ub[:, cc, 16:32],
                                                      op=ALU.subtract),
              [t16[:, cc, :], ub[:, cc, 16:32]], [pl[:, cc, 0:16]])
        for dd in range(2):
            ch = 2 * g + dd
            ms = pmst[dd]
            for tg in range(2):
                bank = nextbank()
                pb = PS(bank)
                for cc in range(2):
                    V("pe", lambda e, cc=cc, g=g, dd=dd, tg=tg, pb=pb: e.matmul(pb, lhsT=wpb[:, g, cc, dd * 128:(dd + 1) * 128],
                                                                                 rhs=pl[:, cc, tg * 512:(tg + 1) * 512],
                                                                                 start=(cc == 0), stop=(cc == 1)),
                      [wpb[:, g, cc, dd * 128:(dd + 1) * 128], pl[:, cc, tg * 512:(tg + 1) * 512]], [pb])
                V("dve", lambda e, ms=ms, tg=tg, pb=pb, ch=ch, dd=dd, g2p=g2p: e.scalar_tensor_tensor(
                    out=ms[:, tg * 512:(tg + 1) * 512], in0=pb, scalar=psh[:, ch:ch + 1],
                    in1=g2p[:, dd, tg * 512:(tg + 1) * 512], op0=ALU.mult, op1=ALU.mult),
                  [pb, psh[:, ch:ch + 1], g2p[:, dd, tg * 512:(tg + 1) * 512]], [ms[:, tg * 512:(tg + 1) * 512]])
            kb.dma("sp", MIXT[8 + ch], ms, "pmst%d" % dd)

    if not want("S"):
        kb.finish()
        return nc
    bs = Bump(R0, R1)
    zq = bs(1024 * 4, F32)
    zk = bs(1024 * 4, F32)
    zv = bs(1024 * 4, F32)
    zga = bs(1024 * 4, F32)
    zgp = bs(1024 * 4, F32)
    uext = bs(1024 * 4, F32)
    qb = [bs(1024 * 4, F32) for _ in range(2)]
    Kc = [bs(1024 * 4, F32) for _ in range(2)]
    Vc = [bs(1024 * 4, F32) for _ in range(2)]
    prod = bs(1024 * 4, F32)
    wv = [bs(1024 * 4, F32) for _ in range(2)]
    sc = [bs(16 * 4, F32) for _ in range(2)]
    pr = [bs(16 * 4, F32) for _ in range(2)]
    psr = bs(1024 * 4, F32)
    mixs = bs(D * 4, F32)
    mixsb = bs(D * 2)
    recs = bs(16 * 4, F32)
    tmp4 = bs(1024 * 4, F32)
    plS = bs(8 * 4 * 2).rearrange("p (c t) -> p c t", c=8)

    kb.dma("sp", zq[0:4, :], ZS[:, 0:1024], "zq")
    kb.dma("sp", zk[0:4, :], ZS[:, 1024:2048], "zk")
    kb.dma("sp", zv[0:4, :], ZS[:, 2048:3072], "zv")
    kb.dma("sp", zga[0:4, :], ZS[:, 3072:4096], "zga")
    kb.dma("sp", uext[15:19, :], ZS[:, 4096:5120], "zu")
    kb.dma("sp", uext[0:15, :], spool, "zu")
    kb.dma("sp", zgp[0:4, :], ZS[:, 5120:6144], "zgp")
    kb.dma("sp", psr[0:4, :], psrow.broadcast_to([4, 1024]), "psr")
    kb.dma("sp", k_samp[2044:2048, :], ZS[:, 1024:2048], "kcopy")
    kb.dma("sp", v_samp[2044:2048, :], ZS[:, 2048:3072], "vcopy")
    kb.dma("sp", pool_s[11:15, :], ZS[:, 4096:5120], "pscopy")

    num_ps = [PS(0)[0:4, :], PS(1)[0:4, :]]
    den_ps = PS(2)[0:4, 0:16]
    pairs = []
    for i in range(4):
        pairs.append((i, "new", 0, 0))
        pairs.append((i, "c", 1920, 1))
        pairs.append((i, "c", 1536 + i, 4))
        pairs.append((i, "c", i, 16))
    npairs = len(pairs)
    for pi, (i, kind, r0, d) in enumerate(pairs):
        first = pi == 0
        last = pi == npairs - 1
        if kind == "new" or pi == 0:
            pass
        if kind == "new":
            qbi = qb[i % 2]
            kb.dma("sp", qbi, ZS[i:i + 1, 0:1024].broadcast_to([128, 1024]), "qb%d" % (i % 2))
            np_ = 4
            kt_ = zk[0:4, :]
            vt_ = zv[0:4, :]
        else:
            qbi = qb[i % 2]
            np_ = 128
            kt_ = Kc[pi % 2]
            vt_ = Vc[pi % 2]
            kb.dma("sp", kt_, ck[r0: r0 + d * 127 + 1: d, :], "kc%d" % (pi % 2))
            kb.dma("sp", vt_, cv[r0: r0 + d * 127 + 1: d, :], "vc%d" % (pi % 2))
        pd_ = prod[0:np_, :]
        V("dve", lambda e, pd_=pd_, kt_=kt_, qbi=qbi, np_=np_: e.tensor_tensor(out=pd_, in0=kt_, in1=qbi[0:np_, :], op=ALU.mult),
          [kt_, qbi[0:np_, :]], [pd_])
        s_ = sc[pi % 2][0:np_, :]
        V("dve", lambda e, s_=s_, pd_=pd_: e.tensor_reduce(out=s_, in_=pd_.rearrange("p (h d) -> p h d", h=16), axis=AX.X,
                                                           op=ALU.add),
          [pd_], [s_])
        p_ = pr[pi % 2][0:np_, :]
        V("act", lambda e, p_=p_, s_=s_: e.activation(out=p_, in_=s_, func=AF.Exp, scale=0.125), [s_], [p_])
        if kind == "new":
            mcol = cst[0:4, C_MNEW + i:C_MNEW + i + 1]
            V("dve", lambda e, p_=p_, mcol=mcol: e.tensor_scalar(out=p_, in0=p_, scalar1=mcol, scalar2=None, op0=ALU.mult),
              [p_, mcol], [p_])
        elif d == 1:
            mcol = cst[:, C_MASK1 + i:C_MASK1 + i + 1]
            V("dve", lambda e, p_=p_, mcol=mcol: e.tensor_scalar(out=p_, in0=p_, scalar1=mcol, scalar2=None, op0=ALU.mult),
              [p_, mcol], [p_])
        w_ = wv[pi % 2][0:np_, :]
        V("dve", lambda e, w_=w_, vt_=vt_, p_=p_: e.tensor_tensor(out=w_.rearrange("p (h d) -> p h d", h=16),
                                                                  in0=vt_.rearrange("p (h d) -> p h d", h=16),
                                                                  in1=p_.unsqueeze(2).broadcast_to([p_.shape[0], 16, 64]),
                                                                  op=ALU.mult),
          [vt_, p_], [w_])
        sel = cst[0:np_, C_SEL + 4 * i:C_SEL + 4 * i + 4]
        for hh in range(2):
            V("pe", lambda e, hh=hh, sel=sel, w_=w_, first=first, last=last: e.matmul(num_ps[hh], lhsT=sel,
                                                                                       rhs=w_[:, hh * 512:(hh + 1) * 512],
                                                                                       start=first, stop=last),
              [sel, w_[:, hh * 512:(hh + 1) * 512]], [num_ps[hh]])
        V("pe", lambda e, sel=sel, p_=p_, first=first, last=last: e.matmul(den_ps, lhsT=sel, rhs=p_, start=first, stop=last),
          [sel, p_], [den_ps])

    V("dve", lambda e: e.reciprocal(out=recs[0:4, :], in_=den_ps), [den_ps], [recs[0:4, :]])
    for hh in range(2):
        o = mixs[0:4, hh * 512:(hh + 1) * 512].rearrange("p (h d) -> p h d", h=8)
        rr = recs[0:4, hh * 8:(hh + 1) * 8]
        V("dve", lambda e, o=o, hh=hh, rr=rr: e.tensor_tensor(out=o, in0=num_ps[hh].rearrange("p (h d) -> p h d", h=8),
                                                              in1=rr.unsqueeze(2).broadcast_to([4, 8, 64]), op=ALU.mult),
          [num_ps[hh], rr], [o])

    def gate_rows(z, dst):
        V("act", lambda e: e.activation(out=tmp4[0:4, :], in_=z, func=AF.Tanh, scale=0.5), [z], [tmp4[0:4, :]])
        V("dve", lambda e: e.scalar_tensor_tensor(out=tmp4[0:4, :], in0=tmp4[0:4, :], scalar=1.0, in1=z, op0=ALU.add, op1=ALU.mult),
          [tmp4[0:4, :], z], [tmp4[0:4, :]])
        V("dve", lambda e: e.scalar_tensor_tensor(out=dst, in0=dst, scalar=0.5, in1=tmp4[0:4, :], op0=ALU.mult, op1=ALU.mult),
          [dst, tmp4[0:4, :]], [dst])

    gate_rows(zga[0:4, :], mixs[0:4, 0:1024])
    pls_ps = PS(3)[:, 0:32].rearrange("p (c t) -> p c t", c=8)
    for ch in range(8):
        g = ch // 2
        pm = cst[0:19, C_PM + 4 * g:C_PM + 4 * g + 4]
        V("pe", lambda e, ch=ch, pm=pm: e.matmul(pls_ps[:, ch, :], lhsT=uext[0:19, ch * 128:(ch + 1) * 128], rhs=pm,
                                                 start=True, stop=True),
          [uext[0:19, ch * 128:(ch + 1) * 128], pm], [pls_ps[:, ch, :]])
    V("dve", lambda e: e.tensor_copy(out=plS, in_=pls_ps), [pls_ps], [plS])
    po_ps = PS(4)[0:4, :]
    po_ps2 = PS(5)[0:4, :]
    for g in range(4):
        dst = (po_ps if g < 2 else po_ps2)[:, (g % 2) * 256:(g % 2 + 1) * 256]
        for cc in range(2):
            V("pe", lambda e, g=g, cc=cc, dst=dst: e.matmul(dst, lhsT=plS[:, 2 * g + cc, :], rhs=wpb[:, g, cc, :],
                                                            start=(cc == 0), stop=(cc == 1)),
              [plS[:, 2 * g + cc, :], wpb[:, g, cc, :]], [dst])
    for half, pp_ in enumerate((po_ps, po_ps2)):
        dst = mixs[0:4, 1024 + half * 512:1024 + (half + 1) * 512]
        V("dve", lambda e, dst=dst, pp_=pp_, half=half: e.tensor_tensor(out=dst, in0=pp_[:, 0:512],
                                                                        in1=psr[0:4, half * 512:(half + 1) * 512], op=ALU.mult),
          [pp_[:, 0:512], psr[0:4, half * 512:(half + 1) * 512]], [dst])
    gate_rows(zgp[0:4, :], mixs[0:4, 1024:2048])
    V("dve", lambda e: e.tensor_copy(out=mixsb[0:4, :], in_=mixs[0:4, :]), [mixs[0:4, :]], [mixsb[0:4, :]])
    pstm = PS(6, BF16)[:, 0:64].rearrange("p (c t) -> p c t", c=16)
    for c in range(16):
        V("pe", lambda e, c=c: e.transpose(out=pstm[:, c, :], in_=mixsb[0:4, c * 128:(c + 1) * 128], identity=ident[0:4, 0:4]),
          [mixsb[0:4, c * 128:(c + 1) * 128], ident[0:4, 0:4]], [pstm[:, c, :]])
    V("dve", lambda e: e.tensor_copy(out=mixTs, in_=pstm), [pstm], [mixTs])

    if not want("E"):
        kb.finish()
        return nc
    be = Bump(R0, R1)
    mixT = be(16 * NOWN * 2).rearrange("p (e t) -> p e t", e=16)
    xtE = [be(D * 4, F32), be(D * 4, F32)]
    gBE = be(D * 4, F32)
    junkE = be(D * 2)
    ystE2 = [be(D * 4, F32), be(D * 4, F32)]
    for e_ in range(16):
        kb.dma("sp", mixT[:, e_, :], MIXT[e_], "mixT")
    kb.dma("sp", gBE, fg.broadcast_to([128, D]), "gBE")

    def out_rows(lhs_of_e, np_, xtile, col, ssb, rsb, ydst, yout, slot, bank0):
        for n4 in range(4):
            pb = PS(bank0 + n4)[0:np_, :]
            for e_ in range(16):
                V("pe", lambda e, e_=e_, n4=n4, pb=pb: e.matmul(pb, lhsT=lhs_of_e(e_), rhs=Wo[:, e_, n4 * 512:(n4 + 1) * 512],
                                                                 start=(e_ == 0), stop=(e_ == 15)),
                  [lhs_of_e(e_), Wo[:, e_, n4 * 512:(n4 + 1) * 512]], [pb])
            xsl = xtile[:, n4 * 512:(n4 + 1) * 512]
            V("dve", lambda e, pb=pb, xsl=xsl: e.tensor_tensor(out=xsl, in0=pb, in1=xsl, op=ALU.add), [pb, xsl], [xsl])
        norm_tile(None, xtile, np_, col, ssb, rsb, gBE, ydst, "e", junk_=junkE)
        kb.dma("sp", yout, ydst, slot)

    def x_load(tt):
        kb.dma("sp", xtE[tt % 2], xall[NHALO + tt * 128:NHALO + (tt + 1) * 128, :], "xtE%d" % (tt % 2))

    x_load(0)
    out_rows(lambda e_: mixTs[:, e_, :], 4, xs_sb[0:4, :], 9, ss2, rs2, ystE2[1][0:4, :], y_s, "ystE1", 4)
    for tt in range(8):
        xtile = xtE[tt % 2]
        if tt + 1 < 8:
            x_load(tt + 1)
        out_rows(lambda e_, tt=tt: mixT[:, e_, tt * 128:(tt + 1) * 128], 128, xtile, tt, ss2, rs2, ystE2[tt % 2],
                 y_own[tt * 128:(tt + 1) * 128, :], "ystE%d" % (tt % 2), 4 * (tt % 2))

    kb.finish()
    return nc


_NC_CACHE = {}


def _consts_for(chunk):
    c = np.zeros((128, NCONST), np.float32)
    k = np.arange(128)[:, None]
    col = np.arange(256)[None, :]
    M = ((col - k >= 0) & (col - k <= 128)).astype(np.float32)
    v0 = M.copy()
    v1 = M.copy()
    if chunk == 0:
        v1[:, 128:] = 0.0
    v2 = M.copy()
    kmin = {0: 128, 1: 64}.get(chunk, 0)
    v2[:kmin, 128:] = 0.0
    c[:, C_MASK:C_MASK + 256] = v0
    c[:, C_MASK + 256:C_MASK + 512] = v1
    c[:, C_MASK + 512:C_MASK + 768] = v2
    c[:, C_IDENT:C_IDENT + 128] = np.eye(128, dtype=np.float32)
    for i in range(4):
        c[:, C_SEL + 4 * i + i] = 1.0
        c[:, C_MASK1 + i] = (np.arange(128) >= i).astype(np.float32)
        for ip in range(4):
            c[ip, C_MNEW + i] = 3.0 if ip == i else (1.0 if ip < i else 0.0)
    s = 1024 * chunk
    for g in range(4):
        w = 2 ** (g + 1)
        for i in range(16):
            c[:, C_INVC + g * 16 + i] = 1.0 / min(s + i + 1, w)
        for i in range(4):
            for row in range(19):
                val = 0.0
                if 15 + i - w < row <= 15 + i:
                    val += 1.0 / w
                if row == 15 + i:
                    val -= 1.0
                c[row, C_PM + 4 * g + i] = val
    return c


def kernel(x_prompt, x_sample, cache_k, cache_v, state_pool, norm_g, w_in, w_pool, pool_scale, w_out,
           final_norm_g):
    f = np.float32
    x_prompt = np.asarray(x_prompt, f)
    x_sample = np.asarray(x_sample, f)
    cache_k = np.asarray(cache_k, f)
    cache_v = np.asarray(cache_v, f)
    state_pool = np.asarray(state_pool, f)
    if "nc" not in _NC_CACHE:
        _NC_CACHE["nc"] = build_nc()
    nc = _NC_CACHE["nc"]
    win = np.ascontiguousarray(np.asarray(w_in, f)[0])
    wpool = np.ascontiguousarray(np.asarray(w_pool, f)[0])
    wout = np.ascontiguousarray(np.asarray(w_out, f)[0])
    ps = np.asarray(pool_scale, f)[0]
    psfm = np.ascontiguousarray(ps.reshape(8, 128).T)
    psrow = np.ascontiguousarray(ps.reshape(1, 1024))
    ng = np.ascontiguousarray(np.asarray(norm_g, f)[0].reshape(1, D))
    fgv = np.ascontiguousarray(np.asarray(final_norm_g, f).reshape(1, D))
    in_maps = []
    for core in range(8):
        b, c = core // 4, core % 4
        s = 1024 * c
        xall = np.zeros((NT, D), f)
        lo = s - NHALO
        src_lo = max(lo, 0)
        xall[src_lo - lo:, :] = x_prompt[b, src_lo:s + NOWN, :]
        in_maps.append({
            "xall": xall,
            "xs": np.ascontiguousarray(x_sample[core]),
            "ck": np.ascontiguousarray(cache_k[0, core].reshape(2048, 1024)),
            "cv": np.ascontiguousarray(cache_v[0, core].reshape(2048, 1024)),
            "spool": np.ascontiguousarray(state_pool[0, core]),
            "ng": ng, "fg": fgv, "win": win, "wpool": wpool, "psfm": psfm, "psrow": psrow, "wout": wout,
            "consts": _consts_for(c),
        })
    res = run_bass_kernel_spmd(nc, in_maps, core_ids=list(range(8)))
    R = res.results
    y_prompt = np.zeros((2, 4096, D), f)
    k_prompt = np.zeros((1, 2, 2048, 16, 64), f)
    v_prompt = np.zeros((1, 2, 2048, 16, 64), f)
    pool_prompt = np.zeros((1, 2, 15, 1024), f)
    y_sample = np.zeros((8, 4, D), f)
    k_sample = np.zeros((1, 8, 2048, 16, 64), f)
    v_sample = np.zeros((1, 8, 2048, 16, 64), f)
    pool_sample = np.zeros((1, 8, 15, 1024), f)
    for core in range(8):
        b, c = core // 4, core % 4
        r = R[core]
        y_prompt[b, 1024 * c:1024 * (c + 1)] = r["y_own"]
        if c >= 2:
            k_prompt[0, b, 1024 * (c - 2):1024 * (c - 1)] = r["k_own"].reshape(1024, 16, 64)
            v_prompt[0, b, 1024 * (c - 2):1024 * (c - 1)] = r["v_own"].reshape(1024, 16, 64)
        if c == 3:
            pool_prompt[0, b] = r["pool_p"][1:16]
        y_sample[core] = r["y_s"]
        k_sample[0, core] = r["k_samp"].reshape(2048, 16, 64)
        v_sample[0, core] = r["v_samp"].reshape(2048, 16, 64)
        pool_sample[0, core] = r["pool_s"]
    return (y_prompt, y_sample, k_prompt, v_prompt, pool_prompt, k_sample, v_sample, pool_sample)
```

```python
import numpy as np
import concourse.bass as bass
import concourse.mybir as mybir
from concourse.bass_utils import run_bass_kernel_spmd

F32 = mybir.dt.float32
BF16 = mybir.dt.bfloat16
AF = mybir.ActivationFunctionType
ALU = mybir.AluOpType
AX = mybir.AxisListType

D = 2048
NOWN = 1024
NHALO = 2048
NT = NOWN + NHALO
EPS = 1e-6
SQD = float(np.sqrt(2048.0))
ARENA_BYTES = 206 * 1024
PSUM_BYTES = 16 * 1024
GR = 64

C_MASK = 0
C_IDENT = 768
C_SEL = 896
C_MASK1 = 912
C_MNEW = 916
C_INVC = 920
C_PM = 984
NCONST = 1000


class Op:
    __slots__ = ("eng", "fn", "waits", "is_dma", "dsem", "dtarget", "milestone", "mcount", "raw", "other", "dent", "dmaw")

    def __init__(self, eng, fn, is_dma=False):
        self.eng = eng
        self.fn = fn
        self.is_dma = is_dma
        self.dsem = None
        self.dtarget = 0
        self.milestone = False
        self.mcount = 0
        self.raw = set()
        self.other = set()
        self.dent = None
        self.dmaw = {}


class KB:
    ENGS = ("pe", "act", "dve", "pool", "sp")

    def __init__(self, nc):
        self.nc = nc
        self.ops = []
        self.state = {}
        self.dsems = {}
        self.esem = {e: nc.alloc_semaphore("sem_" + e) for e in ("pe", "act", "dve", "pool")}

    def _ivals(self, ap):
        name = ap.tensor.name
        sz = mybir.dt.size(ap.dtype)
        dims = list(ap.ap)
        off = ap.offset
        if name == "arena":
            rowlen = ARENA_BYTES // sz
            off = off % rowlen
            dims = dims[1:]
        elif name == "psum":
            rowlen = PSUM_BYTES // sz
            off = off % rowlen
            dims = dims[1:]
        dims = [(abs(s), c) for (s, c) in dims if c > 1 and s != 0]
        dims.sort(key=lambda sc: -sc[0])
        starts = [off]
        k = 0
        prod = 1
        while k < len(dims) - 1 and prod * dims[k][1] <= 64:
            s, c = dims[k]
            starts = [st + i * s for st in starts for i in range(c)]
            prod *= c
            k += 1
        span = sum(s * (c - 1) for s, c in dims[k:]) + 1
        return name, [(st * sz, (st + span) * sz) for st in starts]

    def _grans(self, ap):
        name, ivs = self._ivals(ap)
        st = self.state.setdefault(name, {})
        out = []
        seen = set()
        gr = 2048 if name == "psum" else GR
        for lo, hi in ivs:
            for g in range(lo // gr, (hi - 1) // gr + 1):
                if g not in seen:
                    seen.add(g)
                    out.append(g)
        return st, out

    def _add(self, op, reads, writes):
        writes = list(writes) + [a for a in reads if a.tensor.name == "psum"]
        reads = [a for a in reads if a.tensor.name != "psum"]
        rg = [self._grans(a) for a in reads]
        wg = [self._grans(a) for a in writes]
        for st, gs in rg:
            for g in gs:
                rec = st.get(g)
                if rec is not None and rec[0] is not None:
                    op.raw.add(rec[0])
        for st, gs in wg:
            for g in gs:
                rec = st.get(g)
                if rec is not None:
                    if rec[0] is not None:
                        op.other.add(rec[0])
                    op.other.update(rec[1].values())
                    op.other.update(rec[2])
        for d in list(op.raw) + list(op.other):
            if d.is_dma:
                op.dmaw[id(d.dsem)] = (d.dsem, d.dent[1])
        for st, gs in rg:
            for g in gs:
                rec = st.get(g)
                if rec is None:
                    rec = [None, {}, []]
                    st[g] = rec
                if op.is_dma:
                    rec[2].append(op)
                else:
                    rec[1][op.eng] = op
        for st, gs in wg:
            for g in gs:
                st[g] = [op, {}, []]
        self.ops.append(op)
        return op

    def op(self, eng, fn, reads=(), writes=()):
        return self._add(Op(eng, fn), list(reads), list(writes))

    def dma(self, queue, out, in_, slot, **kw):
        o = Op(queue, None, is_dma=True)
        if slot not in self.dsems:
            self.dsems[slot] = [self.nc.alloc_semaphore("d_" + slot), 0]
        ent = self.dsems[slot]
        o.dsem = ent[0]
        o.dent = ent
        o.fn = lambda e: e.dma_start(out=out, in_=in_, **kw)
        self._add(o, [in_], [out])
        ent[1] += 16
        return o

    def finish(self):
        nc = self.nc
        for op in self.ops:
            deps = set()
            for d in op.raw:
                if d.is_dma or op.is_dma:
                    deps.add(d)
                elif d.eng == "pe" and op.eng == "pe":
                    continue
                else:
                    deps.add(d)
            for d in op.other:
                if d.is_dma or op.is_dma:
                    deps.add(d)
                elif d.eng == op.eng:
                    continue
                else:
                    deps.add(d)
            op.waits = deps
            for d in deps:
                if not d.is_dma:
                    d.milestone = True
        cnt = {e: 0 for e in self.ENGS}
        for op in self.ops:
            if not op.is_dma and op.milestone:
                cnt[op.eng] += 1
                op.mcount = cnt[op.eng]
        streams = {e: [] for e in self.ENGS}
        for op in self.ops:
            streams[op.eng].append(op)
        finals = [(ent[0], ent[1]) for ent in self.dsems.values()]

        def emit(ename, e):
            waited = {}
            for op in streams[ename]:
                need = {}
                for d in op.waits:
                    if d.is_dma:
                        key = id(d.dsem)
                        sem, val = op.dmaw[key]
                    else:
                        sem = self.esem[d.eng]
                        key, val = id(sem), d.mcount
                    if need.get(key, (None, 0))[1] < val:
                        need[key] = (sem, val)
                for key, (sem, val) in need.items():
                    if waited.get(key, 0) < val:
                        e.wait_ge(sem, val)
                        waited[key] = val
                ins = op.fn(e)
                if op.is_dma:
                    ins.then_inc(op.dsem, 16)
                elif op.milestone:
                    ins.then_inc(self.esem[ename], 1)
            if ename == "sp":
                for sem, val in finals:
                    e.wait_ge(sem, val)

        with nc.Block() as block:
            @block.tensor
            def _(e):
                emit("pe", e)

            @block.scalar
            def _(e):
                emit("act", e)

            @block.vector
            def _(e):
                emit("dve", e)

            @block.gpsimd
            def _(e):
                emit("pool", e)

            @block.sync
            def _(e):
                emit("sp", e)


def build_nc(upto="E", dbg=False):
    nc = bass.Bass("TRN2", target_bir_lowering=False)
    kb = KB(nc)
    order = "0ABCDSE"

    def want(stage):
        return order.index(stage) <= order.index(upto)

    def din(name, shape, dt=F32):
        return nc.dram_tensor(name, list(shape), dt, kind="ExternalInput").ap()

    def dout(name, shape, dt=F32):
        return nc.dram_tensor(name, list(shape), dt, kind="ExternalOutput").ap()

    def dscr(name, shape, dt):
        if dbg:
            return nc.dram_tensor(name, list(shape), dt, kind="ExternalOutput").ap()
        return nc.dram_tensor(name, list(shape), dt).ap()

    xall = din("xall", [NT, D])
    xs = din("xs", [4, D])
    ck = din("ck", [2048, 1024])
    cv = din("cv", [2048, 1024])
    spool = din("spool", [15, 1024])
    ng = din("ng", [1, D])
    fg = din("fg", [1, D])
    win = din("win", [D, 6144])
    wpool = din("wpool", [4, 256, 256])
    psfm = din("psfm", [128, 8])
    psrow = din("psrow", [1, 1024])
    wout = din("wout", [D, D])
    consts = din("consts", [128, NCONST])

    y_own = dout("y_own", [NOWN, D])
    k_own = dout("k_own", [NOWN, 1024])
    v_own = dout("v_own", [NOWN, 1024])
    pool_p = dout("pool_p", [16, 1024])
    y_s = dout("y_s", [4, D])
    k_samp = dout("k_samp", [2048, 1024])
    v_samp = dout("v_samp", [2048, 1024])
    pool_s = dout("pool_s", [15, 1024])

    QT = dscr("QT", [8, 128, NOWN], BF16)
    KTs = dscr("KTs", [8, 128, NT], BF16)
    VTs = dscr("VTs", [8, 128, NT], BF16)
    G2A = dscr("G2A", [8, 128, NOWN], BF16)
    UT = dscr("UT", [8, 128, 1040], F32)
    G2P = dscr("G2P", [8, 128, NOWN], BF16)
    MIXT = dscr("MIXT", [16, 128, NOWN], BF16)
    ZS = dscr("ZS", [4, 6144], F32)

    arena = nc.alloc_sbuf_tensor("arena", [128, ARENA_BYTES // 2], BF16)
    psum = nc.alloc_psum_tensor("psum", [128, PSUM_BYTES // 4], F32)

    def A(off, nbytes, dt=BF16):
        assert off % 4 == 0 and off + nbytes <= ARENA_BYTES, (off, nbytes)
        v = arena[:, off // 2:(off + nbytes) // 2]
        if dt != BF16:
            v = v.bitcast(dt)
        return v

    def PS(bank, dt=F32):
        v = psum[:, bank * 512:(bank + 1) * 512]
        if dt != F32:
            v = v.bitcast(dt)
        return v

    class Bump:
        def __init__(self, lo, hi):
            self.p = lo
            self.hi = hi

        def __call__(self, nbytes, dt=BF16):
            nb = (nbytes + 63) // 64 * 64
            v = A(self.p, nbytes, dt)
            self.p += nb
            assert self.p <= self.hi, (self.p, self.hi)
            return v

    KBY = 1024
    R0 = 0
    R1 = 96 * KBY
    R2 = R1 + 32 * KBY
    R3 = R2 + 32 * KBY

    b3 = Bump(R3, ARENA_BYTES)
    cst = b3(NCONST * 4, F32)
    MB = b3(3 * 2 * 2 * 128 * 2).rearrange("p (v j h c) -> p v j h c", v=3, j=2, h=2)
    ident = b3(128 * 2)
    ones = b3(128 * 2)
    ss = b3(32 * 4, F32)
    rs = b3(32 * 4, F32)
    ss2 = b3(16 * 4, F32)
    rs2 = b3(16 * 4, F32)
    psf = b3(8 * 4, F32)
    psh = b3(8 * 4, F32)
    wpb = b3(4 * 2 * 256 * 2).rearrange("p (g c d) -> p g c d", g=4, c=2)
    xnTs = b3(16 * 4 * 2).rearrange("p (c t) -> p c t", c=16)
    mixTs = b3(16 * 4 * 2).rearrange("p (c t) -> p c t", c=16)
    zst = b3(1024 * 4, F32)
    ppst = b3(1024 * 4, F32)
    stg = [b3(6 * KBY), b3(6 * KBY)]
    thb = [b3(512 * 4, F32), b3(512 * 4, F32)]
    xs_sb = b3(D * 4, F32)
    wslot = []
    b1 = Bump(R1, R2)
    ws32 = [b1(16 * 128 * 4, F32).rearrange("p (c n) -> p c n", c=16) for _ in range(2)]
    for i in range(4):
        wslot.append(b1(16 * 128 * 2).rearrange("p (c n) -> p c n", c=16))
    b2 = Bump(R2, R3)
    xt = [b2(D * 4, F32), b2(D * 4, F32)]
    gB = b2(D * 4, F32)
    xn2 = [b2(D * 2), b2(D * 2)]
    xn = xn2[0]
    junk = ppst.bitcast(BF16)

    xnT = A(R0, 16 * NT * 2).rearrange("p (c t) -> p c t", c=16)

    V = kb.op

    kb.dma("sp", cst, consts, "cst")
    cm4 = cst[:, C_MASK:C_MASK + 768].rearrange("p (v j c) -> p v j c", v=3, j=2)
    for h in range(2):
        V("dve", lambda e, h=h: e.tensor_scalar(out=MB[:, :, :, h, :], in0=cm4, scalar1=-1.0, scalar2=30000.0,
                                                op0=ALU.add, op1=ALU.mult),
          [cst], [MB[:, :, :, h, :]])
    V("dve", lambda e: e.tensor_copy(out=ident, in_=cst[:, C_IDENT:C_IDENT + 128]), [cst], [ident])
    V("dve", lambda e: e.memset(ones, 1.0), [], [ones])
    kb.dma("sp", psf, psfm, "psf")
    V("dve", lambda e: e.tensor_scalar(out=psh, in0=psf, scalar1=0.5, scalar2=None, op0=ALU.mult), [psf], [psh])
    kb.dma("pool", wpb, wpool.rearrange("g (c p) d -> p g c d", p=128), "wpb")
    kb.dma("sp", gB, ng.broadcast_to([128, D]), "gB")

    if not want("A"):
        kb.finish()
        return nc
    def norm_tile(src_ap, xtile, np_, col, ssb, rsb, gain, dst, tag, junk_=None):
        jk = junk if junk_ is None else junk_
        V("act", lambda e: e.activation(out=jk[0:np_, :], in_=xtile, func=AF.Square,
                                        accum_out=ssb[0:np_, col:col + 1]),
          [xtile], [jk[0:np_, :], ssb[0:np_, col:col + 1]])
        V("act", lambda e: e.activation(out=rsb[0:np_, col:col + 1], in_=ssb[0:np_, col:col + 1], func=AF.Sqrt,
                                        scale=1.0 / D, bias=EPS),
          [ssb[0:np_, col:col + 1]], [rsb[0:np_, col:col + 1]])
        V("dve", lambda e: e.reciprocal(out=rsb[0:np_, col:col + 1], in_=rsb[0:np_, col:col + 1]),
          [rsb[0:np_, col:col + 1]], [rsb[0:np_, col:col + 1]])
        V("dve", lambda e: e.scalar_tensor_tensor(out=dst, in0=xtile, scalar=rsb[0:np_, col:col + 1],
                                                  in1=gain[0:np_, :], op0=ALU.mult, op1=ALU.mult),
          [xtile, rsb[0:np_, col:col + 1], gain[0:np_, :]], [dst])

    a_cnt = [0]

    xcur = {}

    xslot = {}

    def x_issue(t):
        if t in xslot:
            return
        n_ = a_cnt[0]
        a_cnt[0] += 1
        xslot[t] = n_
        kb.dma("sp", xt[n_ % 2], xall[t * 128:(t + 1) * 128, :], "xt%d" % (n_ % 2))

    def a_norm(t):
        x_issue(t)
        n_ = xslot[t]
        xtile = xt[n_ % 2]
        xcur[t] = xn2[n_ % 2]
        norm_tile(None, xtile, 128, t, ss, rs, gB, xcur[t], "a")

    def a_trans(t, bank0):
        xn = xcur[t]
        for half in range(2):
            pst = PS(bank0 + half, BF16).rearrange("p (c t) -> p c t", c=8)
            for cc in range(8):
                c = half * 8 + cc
                V("pe", lambda e, c=c, cc=cc, pst=pst, xn=xn: e.transpose(out=pst[:, cc, :], in_=xn[:, c * 128:(c + 1) * 128],
                                                                         identity=ident),
                  [xn[:, c * 128:(c + 1) * 128], ident], [pst[:, cc, :]])
            dst = xnT[:, half * 8:half * 8 + 8, t * 128:(t + 1) * 128]
            if half == 0:
                V("act", lambda e, dst=dst, pst=pst: e.activation(out=dst, in_=pst, func=AF.Copy), [pst], [dst])
            else:
                V("dve", lambda e, dst=dst, pst=pst: e.tensor_copy(out=dst, in_=pst), [pst], [dst])

    def a_tile(t, bank0):
        a_norm(t)
        a_trans(t, bank0)

    sc_order = [0, 1, 6, 7, 2, 8, 3, 9, 4, 10, 5, 11]
    chunk_order = [4 * sc + jj for sc in sc_order for jj in range(4)]
    winv = win.rearrange("(c p) n -> p c n", p=128)

    def w_dma(n_):
        j = chunk_order[n_]
        kb.dma("sp", ws32[n_ % 2], winv[:, :, j * 128:(j + 1) * 128], "ws%d" % (n_ % 2))

    def w_cast(n_):
        V("pool", lambda e, n_=n_: e.tensor_copy(out=wslot[n_ % 4], in_=ws32[n_ % 2]), [ws32[n_ % 2]], [wslot[n_ % 4]])

    x_issue(16)
    x_issue(17)
    w_dma(0)
    w_dma(1)
    w_cast(0)
    w_cast(1)
    own_order = list(range(16, 24)) + [15]
    a_norm(own_order[0])
    for k_, t in enumerate(own_order):
        if k_ + 1 < len(own_order):
            a_norm(own_order[k_ + 1])
        a_trans(t, 2 * (k_ % 2))
    w_dma(2)
    kb.dma("sp", xs_sb[0:4, :], xs, "xs")
    xns = xn[0:4, :]
    norm_tile(None, xs_sb[0:4, :], 4, 30, ss, rs, gB, xns, "s")
    psts = PS(0, BF16)[:, 0:64].rearrange("p (c t) -> p c t", c=16)
    for c in range(16):
        V("pe", lambda e, c=c: e.transpose(out=psts[:, c, :], in_=xn[0:4, c * 128:(c + 1) * 128],
                                           identity=ident[0:4, 0:4]),
          [xn[0:4, c * 128:(c + 1) * 128], ident[0:4, 0:4]], [psts[:, c, :]])
    V("dve", lambda e: e.tensor_copy(out=xnTs, in_=psts), [psts], [xnTs])

    if not want("B"):
        kb.finish()
        return nc
    OWN_G = [(NHALO, 512), (NHALO + 512, 512)]
    ALL_G = [(i * 512, 512) for i in range(6)]
    U_G = [(NHALO - 16, 16)] + OWN_G
    bankrot = [0]

    def nextbank():
        b = 4 + bankrot[0] % 4
        bankrot[0] += 1
        return b

    evac_alt = [0]
    sc_order = [0, 1, 6, 7, 2, 8, 3, 9, 4, 10, 5, 11]
    chunk_order = [4 * sc + jj for sc in sc_order for jj in range(4)]
    halo_left = list(range(14, -1, -1))
    pend = [None]
    pendq = []

    for n_, j in enumerate(chunk_order):
        w = wslot[n_ % 4]
        if len(pendq) == 2 or (pendq and not halo_left):
            a_trans(pendq.pop(0), 0)
        if halo_left:
            pendq.append(halo_left.pop())
            a_norm(pendq[-1])
            if halo_left:
                x_issue(halo_left[-1])
        if n_ + 2 < 48:
            w_cast(n_ + 2)
        if n_ + 3 < 48:
            w_dma(n_ + 3)
        kind = j // 8
        groups = {0: OWN_G, 1: ALL_G, 2: ALL_G, 3: OWN_G, 4: U_G, 5: OWN_G}[kind]
        sb = stg[n_ % 2]
        if kind == 4:
            sview = sb[:, 0:2080].bitcast(F32)
        else:
            sview = sb
        pos = 0
        for (t0, n) in groups:
            bank = nextbank()
            pb = PS(bank)[:, 0:n]
            for c in range(16):
                V("pe", lambda e, c=c, pb=pb, w=w, t0=t0, n=n: e.matmul(pb, lhsT=w[:, c, :], rhs=xnT[:, c, t0:t0 + n],
                                                                       start=(c == 0), stop=(c == 15)),
                  [w[:, c, :], xnT[:, c, t0:t0 + n]], [pb])
            dst = sview[:, pos:pos + n]
            if kind in (0, 1, 2, 4):
                if evac_alt[0] % 2 == 0:
                    V("act", lambda e, dst=dst, pb=pb: e.activation(out=dst, in_=pb, func=AF.Copy), [pb], [dst])
                else:
                    V("dve", lambda e, dst=dst, pb=pb: e.tensor_copy(out=dst, in_=pb), [pb], [dst])
                evac_alt[0] += 1
            else:
                th = thb[evac_alt[0] % 2][:, 0:n]
                evac_alt[0] += 1
                V("act", lambda e, th=th, pb=pb: e.activation(out=th, in_=pb, func=AF.Tanh, scale=0.5), [pb], [th])
                V("dve", lambda e, dst=dst, th=th, pb=pb: e.scalar_tensor_tensor(out=dst, in0=th, scalar=1.0, in1=pb,
                                                                               op0=ALU.add, op1=ALU.mult),
                  [th, pb], [dst])
            pos += n
        dest = {0: QT, 1: KTs, 2: VTs, 3: G2A, 4: UT, 5: G2P}[kind]
        kb.dma("pool", dest[j % 8], sview[:, 0:pos], "stg%d" % (n_ % 2))
        if n_ % 2 == 1:
            wa = wslot[(n_ - 1) % 4]
            assert wslot[n_ % 4].offset - wa.offset == 2048

            def w2(c, wa=wa):
                return bass.AP(wa.tensor, wa.offset + c * 128, [list(wa.ap[0]), [2048, 2], [1, 128]])

            pa = PS(2)[0:4, 0:256].rearrange("p (s n) -> p s n", s=2)
            for c in range(16):
                V("pe", lambda e, c=c, w2=w2, pa=pa: e.matmul(pa, lhsT=xnTs[:, c, :], rhs=w2(c), start=(c == 0), stop=(c == 15)),
                  [xnTs[:, c, :], w2(c)], [pa])
            c0 = ((j - 1) % 8) * 128
            zd = zst[0:4, c0:c0 + 256].rearrange("p (s n) -> p s n", s=2)
            V("dve", lambda e, zd=zd, pa=pa: e.tensor_copy(out=zd, in_=pa), [pa], [zd])
            if j % 4 == 3:
                h4 = (j % 8) // 4
                kb.dma("pool", ZS[:, (j // 4) * 512:(j // 4 + 1) * 512], zst[0:4, h4 * 512:(h4 + 1) * 512], "zst")
            if kind == 4:
                pp = PS(3)[0:16, 0:256].rearrange("p (s n) -> p s n", s=2)
                for c in range(16):
                    V("pe", lambda e, c=c, w2=w2, pp=pp: e.matmul(pp, lhsT=xnT[:, c, NT - 16:NT], rhs=w2(c),
                                                                  start=(c == 0), stop=(c == 15)),
                      [xnT[:, c, NT - 16:NT], w2(c)], [pp])
                pd = ppst[0:16, c0:c0 + 256].rearrange("p (s n) -> p s n", s=2)
                V("dve", lambda e, pd=pd, pp=pp: e.tensor_copy(out=pd, in_=pp), [pp], [pd])
                if j % 4 == 3:
                    h4 = (j % 8) // 4
                    kb.dma("pool", pool_p[:, h4 * 512:(h4 + 1) * 512], ppst[0:16, h4 * 512:(h4 + 1) * 512], "ppst")
    assert not halo_left and not pendq

    if not want("C"):
        kb.finish()
        return nc
    Wo = A(R1, 16 * D * 2).rearrange("p (e n) -> p e n", e=16)
    woutv = wout.rearrange("(e p) n -> p e n", p=128)
    for e4 in range(8):
        for hcol in range(2):
            kb.dma("pool", Wo[:, 2 * e4:2 * e4 + 2, hcol * 1024:(hcol + 1) * 1024],
                   woutv[:, 2 * e4:2 * e4 + 2, hcol * 1024:(hcol + 1) * 1024], "wo")
    for i in range(4):
        r0 = i * 511
        kb.dma("pool", k_samp[r0:r0 + 511, :], ck[4 + r0:4 + r0 + 511, :], "kcopy")
        kb.dma("pool", v_samp[r0:r0 + 511, :], cv[4 + r0:4 + r0 + 511, :], "vcopy")
    kb.dma("pool", pool_s[0:11, :], spool[4:15, :], "pscopy")
    bc = Bump(R0, R1)
    QZ = [bc(2 * NOWN * 2).rearrange("p (h t) -> p h t", h=2) for _ in range(2)]
    KTb = [bc(NT * 2) for _ in range(2)]
    VTb = [bc(NT * 2) for _ in range(2)]
    G2 = [bc(NOWN * 2) for _ in range(2)]
    Vt = bc(53 * 128 * 2).rearrange("p (s f) -> p s f", s=53)
    Eb = [bc(512 * 2) for _ in range(3)]
    Pb = [bc(512 * 2) for _ in range(3)]
    numacc = bc(2 * NOWN * 4, F32).rearrange("p (h t) -> p h t", h=2)
    denacc = bc(2 * NOWN * 4, F32).rearrange("p (h t) -> p h t", h=2)
    rbuf = bc(NOWN * 4, F32)
    r2buf = bc(NOWN * 4, F32)
    mixst = [bc(NOWN * 2) for _ in range(2)]
    kvst = [bc(1024 * 4, F32).rearrange("p (t f) -> p t f", t=8) for _ in range(2)]

    for s_ in range(2):
        V("dve", lambda e, s_=s_: e.memset(QZ[s_][64:128, 0, :], 0.0), [], [QZ[s_][64:128, 0, :]])
        V("dve", lambda e, s_=s_: e.memset(QZ[s_][0:64, 1, :], 0.0), [], [QZ[s_][0:64, 1, :]])

    def vslot(d, r, j):
        if d == 1:
            return j - 15
        if d == 4:
            return 9 + r * 3 + (j - 3)
        return 21 + r * 2 + j

    vt_list = [(1, 0, j, 128) for j in range(15, 24)]
    vt_list += [(4, r, j, 128) for r in range(4) for j in (3, 4, 5)]
    vt_list += [(16, r, j, 128 if j == 0 else 64) for r in range(16) for j in (0, 1)]
    units = [(1, 0, i, 128, 1 if i == 16 else 0) for i in range(16, 24)]
    units += [(4, r, i, 128, 1 if i == 4 else 0) for r in range(4) for i in (4, 5)]
    units += [(16, r, 1, 64, 2) for r in range(16)]
    items = [[u_] for u_ in units[:16]] + [[units[16 + 2 * k], units[17 + 2 * k]] for k in range(8)]

    def load_hp(hp):
        s_ = hp % 2
        kb.dma("sp", QZ[s_][0:64, 0, :], QT[hp, 0:64, :], "q%d" % s_)
        kb.dma("sp", QZ[s_][64:128, 1, :], QT[hp, 64:128, :], "q%d" % s_)
        kb.dma("sp", KTb[s_], KTs[hp], "k%d" % s_)
        kb.dma("sp", VTb[s_], VTs[hp], "v%d" % s_)

    def load_g2(hp):
        kb.dma("sp", G2[hp % 2], G2A[hp], "g%d" % (hp % 2))

    def combine(hp):
        s_ = hp % 2
        ms = mixst[s_]
        g2 = G2[s_]
        for h in range(2):
            lo, hi = 64 * h, 64 * h + 64
            V("act", lambda e, h=h, lo=lo, hi=hi: e.activation(out=r2buf[lo:hi, :], in_=denacc[lo:hi, h, :], func=AF.Ln),
              [denacc[lo:hi, h, :]], [r2buf[lo:hi, :]])
            V("act", lambda e, lo=lo, hi=hi: e.activation(out=rbuf[lo:hi, :], in_=r2buf[lo:hi, :], func=AF.Exp, scale=-1.0),
              [r2buf[lo:hi, :]], [rbuf[lo:hi, :]])
            V("dve", lambda e, h=h, lo=lo, hi=hi: e.scalar_tensor_tensor(out=r2buf[lo:hi, :], in0=numacc[lo:hi, h, :], scalar=0.5,
                                                                       in1=rbuf[lo:hi, :], op0=ALU.mult, op1=ALU.mult),
              [numacc[lo:hi, h, :], rbuf[lo:hi, :]], [r2buf[lo:hi, :]])
            V("dve", lambda e, lo=lo, hi=hi, ms=ms, g2=g2: e.tensor_tensor(out=ms[lo:hi, :], in0=r2buf[lo:hi, :],
                                                                         in1=g2[lo:hi, :], op=ALU.mult),
              [r2buf[lo:hi, :], g2[lo:hi, :]], [ms[lo:hi, :]])
        kb.dma("sp", MIXT[hp], ms, "mixst%d" % s_)

    load_hp(0)
    load_g2(0)
    for hp in range(8):
        s_ = hp % 2
        if hp + 1 < 8:
            load_hp(hp + 1)
        kt = KTb[s_]
        vt = VTb[s_]
        qz = QZ[s_]
        for g0 in range(0, 53, 8):
            grp = vt_list[g0:g0 + 8]
            bank = 5 + (g0 // 8) % 2
            pst = PS(bank, BF16).rearrange("p (s f) -> p s f", s=8)
            for k, (d, r, j, nk) in enumerate(grp):
                src = vt[:, d * 128 * j + r: d * 128 * j + r + d * (nk - 1) + 1: d]
                V("pe", lambda e, k=k, src=src, nk=nk, pst=pst: e.transpose(out=pst[0:nk, k, :], in_=src, identity=ident),
                  [src, ident], [pst[0:nk, k, :]])
            n = len(grp)
            V("dve", lambda e, g0=g0, n=n, pst=pst: e.tensor_copy(out=Vt[:, g0:g0 + n, :], in_=pst[:, 0:n, :]),
              [pst[:, 0:n, :]], [Vt[:, g0:g0 + n, :]])
        pst = PS(6, BF16).rearrange("p (s f) -> p s f", s=8)
        for tt in range(8):
            src = kt[:, NHALO + tt * 128: NHALO + (tt + 1) * 128]
            V("pe", lambda e, tt=tt, src=src, pst=pst: e.transpose(out=pst[:, tt, :], in_=src, identity=ident),
              [src, ident], [pst[:, tt, :]])
        V("dve", lambda e, pst=pst: e.tensor_copy(out=kvst[0], in_=pst), [pst], [kvst[0]])
        kb.dma("sp", k_own[:, hp * 128:(hp + 1) * 128].rearrange("(t p) f -> p t f", p=128), kvst[0], "kvst0")
        V("dve", lambda e: e.tensor_copy(out=kvst[1], in_=Vt[:, 1:9, :]), [Vt[:, 1:9, :]], [kvst[1]])
        kb.dma("sp", v_own[:, hp * 128:(hp + 1) * 128].rearrange("(t p) f -> p t f", p=128), kvst[1], "kvst1")

        if hp >= 1:
            combine(hp - 1)
        if hp + 1 < 8:
            load_g2(hp + 1)
        def qk(u):
            subs = items[u]
            ns = len(subs)
            nq = subs[0][3]
            sb5 = PS(u % 3)[:, 0:ns * 4 * nq].rearrange("p (s j h q) -> p s j h q", s=ns, j=2, h=2)
            for si, (d, r, i, nq_, mv) in enumerate(subs):
                q0 = d * 128 * i + r - NHALO
                qap = qz[:, :, q0: q0 + d * (nq - 1) + 1: d]
                o_all = sb5[:, si]
                mb = MB[:, mv, :, :, 0:nq]
                V("pe", lambda e, o_all=o_all, mb=mb: e.matmul(o_all, lhsT=ident, rhs=mb, start=True, stop=False),
                  [ident, mb], [o_all])
                for jj in range(2):
                    j = i - jj
                    nk = 64 if (d == 16 and j == 1) else 128
                    k0 = d * 128 * j + r
                    kap = kt[:, k0: k0 + d * (nk - 1) + 1: d]
                    o = sb5[0:nk, si, jj, :, :]
                    V("pe", lambda e, o=o, kap=kap, qap=qap, jj=jj: e.matmul(o, lhsT=kap, rhs=qap, start=False, stop=(jj == 1)),
                      [kap, qap], [o])
            sfull = PS(u % 3)[:, 0:ns * 4 * nq]
            eb = Eb[u % 3][:, 0:ns * 4 * nq]
            V("act", lambda e, eb=eb, sfull=sfull: e.activation(out=eb, in_=sfull, func=AF.Exp, scale=0.125),
              [sfull], [eb])

        def pvd(u):
            subs = items[u]
            ns = len(subs)
            nq = subs[0][3]
            bank = (3, 4, 7)[u % 3]
            od = PS(bank)[:, 0:ns * 4 * nq].rearrange("p (s x) -> p s x", s=ns)
            pb3 = Eb[u % 3][:, 0:ns * 4 * nq].rearrange("p (s j hq) -> p s j hq", s=ns, j=2)
            for si, (d, r, i, nq_, mv) in enumerate(subs):
                o_ = od[:, si, 0:2 * nq]
                dn = od[:, si, 2 * nq:4 * nq]
                for jj in range(2):
                    j = i - jj
                    nk = 64 if (d == 16 and j == 1) else 128
                    vs = Vt[0:nk, vslot(d, r, j), :]
                    pm = pb3[0:nk, si, jj, :]
                    V("pe", lambda e, jj=jj, vs=vs, pm=pm, o_=o_: e.matmul(o_, lhsT=vs, rhs=pm, start=(jj == 0), stop=(jj == 1)),
                      [vs, pm], [o_])
                for jj in range(2):
                    j = i - jj
                    nk = 64 if (d == 16 and j == 1) else 128
                    pm = pb3[0:nk, si, jj, :]
                    V("pe", lambda e, jj=jj, nk=nk, pm=pm, dn=dn: e.matmul(dn, lhsT=ones[0:nk, :], rhs=pm,
                                                                          start=(jj == 0), stop=(jj == 1)),
                      [ones[0:nk, :], pm], [dn])
            d, r, i, nq_, mv = subs[0]
            q0 = d * 128 * i + r - NHALO
            if ns == 1:
                na = numacc[:, :, q0: q0 + d * (nq - 1) + 1: d]
                da = denacc[:, :, q0: q0 + d * (nq - 1) + 1: d]
                o3 = od[:, 0, 0:2 * nq].rearrange("p (h q) -> p h q", h=2)
                d3 = od[:, 0, 2 * nq:4 * nq].rearrange("p (h q) -> p h q", h=2)
            else:
                def acc_ap(acc):
                    return bass.AP(acc.tensor, acc.offset + q0, [list(acc.ap[0]), [1, 2], [NOWN, 2], [d, nq]])
                na = acc_ap(numacc)
                da = acc_ap(denacc)
                o3 = od[:, :, 0:2 * nq].rearrange("p s (h q) -> p s h q", h=2)
                d3 = od[:, :, 2 * nq:4 * nq].rearrange("p s (h q) -> p s h q", h=2)
            if d == 1:
                V("dve", lambda e, na=na, o3=o3: e.tensor_copy(out=na, in_=o3), [o3], [na])
                V("dve", lambda e, da=da, d3=d3: e.tensor_copy(out=da, in_=d3), [d3], [da])
            else:
                V("dve", lambda e, na=na, o3=o3: e.tensor_tensor(out=na, in0=o3, in1=na, op=ALU.add), [o3, na], [na])
                V("dve", lambda e, da=da, d3=d3: e.tensor_tensor(out=da, in0=d3, in1=da, op=ALU.add), [d3, da], [da])

        for u in range(len(items)):
            qk(u)
            if u >= 2:
                pvd(u - 2)
        pvd(len(items) - 2)
        pvd(len(items) - 1)

    combine(7)

    if not want("D"):
        kb.finish()
        return nc
    bd = Bump(R0, R1)
    ub4 = [bd(2 * 1040 * 4, F32).rearrange("p (c t) -> p c t", c=2) for _ in range(4)]
    g2p4 = [bd(2 * NOWN * 2).rearrange("p (c t) -> p c t", c=2) for _ in range(4)]
    sA = bd(2 * 1040 * 4, F32).rearrange("p (c t) -> p c t", c=2)
    sB = bd(2 * 1040 * 4, F32).rearrange("p (c t) -> p c t", c=2)
    pl = bd(2 * NOWN * 2).rearrange("p (c t) -> p c t", c=2)
    t16 = bd(2 * 16 * 4, F32).rearrange("p (c t) -> p c t", c=2)
    pmst = [bd(NOWN * 2) for _ in range(2)]
    for g in range(4):
        for cc in range(2):
            kb.dma("sp", ub4[g][:, cc, :], UT[2 * g + cc], "ub%d" % g)
            kb.dma("sp", g2p4[g][:, cc, :], G2P[2 * g + cc], "g2p%d" % g)
    for g in range(4):
        wdw = 2 ** (g + 1)
        ub = ub4[g]
        g2p = g2p4[g]
        cur = ub
        bufs = [sA, sB]
        sh = 1
        for step in range(g + 1):
            nxt = bufs[step % 2]
            lo = 2 * sh - 1
            V("dve", lambda e, nxt=nxt, cur=cur, lo=lo, sh=sh: e.tensor_tensor(out=nxt[:, :, lo:1040], in0=cur[:, :, lo:1040],
                                                                                  in1=cur[:, :, lo - sh:1040 - sh], op=ALU.add),
              [cur[:, :, lo - sh:1040]], [nxt[:, :, lo:1040]])
            cur = nxt
            sh *= 2
        V("dve", lambda e, cur=cur, wdw=wdw, ub=ub: e.scalar_tensor_tensor(out=pl, in0=cur[:, :, 16:1040], scalar=1.0 / wdw,
                                                                     in1=ub[:, :, 16:1040], op0=ALU.mult, op1=ALU.subtract),
          [cur[:, :, 16:1040], ub[:, :, 16:1040]], [pl])
        for cc in range(2):
            iv = cst[:, C_INVC + g * 16:C_INVC + g * 16 + 16]
            V("dve", lambda e, cur=cur, cc=cc, iv=iv: e.tensor_tensor(out=t16[:, cc, :], in0=cur[:, cc, 16:32], in1=iv, op=ALU.mult),
              [cur[:, cc, 16:32], iv], [t16[:, cc, :]])
            V("dve", lambda e, cc=cc, ub=ub: e.tensor_tensor(out=pl[:, cc, 0:16], in0=t16[:, cc, :], in1=ub[:, cc, 16:32],
                                                      op=ALU.subtract),
              [t16[:, cc, :], ub[:, cc, 16:32]], [pl[:, cc, 0:16]])
        for dd in range(2):
            ch = 2 * g + dd
            ms = pmst[dd]
            for tg in range(2):
                bank = nextbank()
                pb = PS(bank)
                for cc in range(2):
                    V("pe", lambda e, cc=cc, g=g, dd=dd, tg=tg, pb=pb: e.matmul(pb, lhsT=wpb[:, g, cc, dd * 128:(dd + 1) * 128],
                                                                                 rhs=pl[:, cc, tg * 512:(tg + 1) * 512],
                                                                                 start=(cc == 0), stop=(cc == 1)),
                      [wpb[:, g, cc, dd * 128:(dd + 1) * 128], pl[:, cc, tg * 512:(tg + 1) * 512]], [pb])
                V("dve", lambda e, ms=ms, tg=tg, pb=pb, ch=ch, dd=dd, g2p=g2p: e.scalar_tensor_tensor(
                    out=ms[:, tg * 512:(tg + 1) * 512], in0=pb, scalar=psh[:, ch:ch + 1],
                    in1=g2p[:, dd, tg * 512:(tg + 1) * 512], op0=ALU.mult, op1=ALU.mult),
                  [pb, psh[:, ch:ch + 1], g2p[:, dd, tg * 512:(tg + 1) * 512]], [ms[:, tg * 512:(tg + 1) * 512]])
            kb.dma("sp", MIXT[8 + ch], ms, "pmst%d" % dd)

    if not want("S"):
        kb.finish()
        return nc
    bs = Bump(R0, R1)
    zq = bs(1024 * 4, F32)
    zk = bs(1024 * 4, F32)
    zv = bs(1024 * 4, F32)
    zga = bs(1024 * 4, F32)
    zgp = bs(1024 * 4, F32)
    uext = bs(1024 * 4, F32)
    qb = [bs(1024 * 4, F32) for _ in range(2)]
    Kc = [bs(1024 * 4, F32) for _ in range(2)]
    Vc = [bs(1024 * 4, F32) for _ in range(2)]
    prod = bs(1024 * 4, F32)
    wv = [bs(1024 * 4, F32) for _ in range(2)]
    sc = [bs(16 * 4, F32) for _ in range(2)]
    pr = [bs(16 * 4, F32) for _ in range(2)]
    psr = bs(1024 * 4, F32)
    mixs = bs(D * 4, F32)
    mixsb = bs(D * 2)
    recs = bs(16 * 4, F32)
    tmp4 = bs(1024 * 4, F32)
    plS = bs(8 * 4 * 2).rearrange("p (c t) -> p c t", c=8)

    kb.dma("sp", zq[0:4, :], ZS[:, 0:1024], "zq")
    kb.dma("sp", zk[0:4, :], ZS[:, 1024:2048], "zk")
    kb.dma("sp", zv[0:4, :], ZS[:, 2048:3072], "zv")
    kb.dma("sp", zga[0:4, :], ZS[:, 3072:4096], "zga")
    kb.dma("sp", uext[15:19, :], ZS[:, 4096:5120], "zu")
    kb.dma("sp", uext[0:15, :], spool, "zu")
    kb.dma("sp", zgp[0:4, :], ZS[:, 5120:6144], "zgp")
    kb.dma("sp", psr[0:4, :], psrow.broadcast_to([4, 1024]), "psr")
    kb.dma("sp", k_samp[2044:2048, :], ZS[:, 1024:2048], "kcopy")
    kb.dma("sp", v_samp[2044:2048, :], ZS[:, 2048:3072], "vcopy")
    kb.dma("sp", pool_s[11:15, :], ZS[:, 4096:5120], "pscopy")

    num_ps = [PS(0)[0:4, :], PS(1)[0:4, :]]
    den_ps = PS(2)[0:4, 0:16]
    pairs = []
    for i in range(4):
        pairs.append((i, "new", 0, 0))
        pairs.append((i, "c", 1920, 1))
        pairs.append((i, "c", 1536 + i, 4))
        pairs.append((i, "c", i, 16))
    npairs = len(pairs)
    for pi, (i, kind, r0, d) in enumerate(pairs):
        first = pi == 0
        last = pi == npairs - 1
        if kind == "new" or pi == 0:
            pass
        if kind == "new":
            qbi = qb[i % 2]
            kb.dma("sp", qbi, ZS[i:i + 1, 0:1024].broadcast_to([128, 1024]), "qb%d" % (i % 2))
            np_ = 4
            kt_ = zk[0:4, :]
            vt_ = zv[0:4, :]
        else:
            qbi = qb[i % 2]
            np_ = 128
            kt_ = Kc[pi % 2]
            vt_ = Vc[pi % 2]
            kb.dma("sp", kt_, ck[r0: r0 + d * 127 + 1: d, :], "kc%d" % (pi % 2))
            kb.dma("sp", vt_, cv[r0: r0 + d * 127 + 1: d, :], "vc%d" % (pi % 2))
        pd_ = prod[0:np_, :]
        V("dve", lambda e, pd_=pd_, kt_=kt_, qbi=qbi, np_=np_: e.tensor_tensor(out=pd_, in0=kt_, in1=qbi[0:np_, :], op=ALU.mult),
          [kt_, qbi[0:np_, :]], [pd_])
        s_ = sc[pi % 2][0:np_, :]
        V("dve", lambda e, s_=s_, pd_=pd_: e.tensor_reduce(out=s_, in_=pd_.rearrange("p (h d) -> p h d", h=16), axis=AX.X,
                                                           op=ALU.add),
          [pd_], [s_])
        p_ = pr[pi % 2][0:np_, :]
        V("act", lambda e, p_=p_, s_=s_: e.activation(out=p_, in_=s_, func=AF.Exp, scale=0.125), [s_], [p_])
        if kind == "new":
            mcol = cst[0:4, C_MNEW + i:C_MNEW + i + 1]
            V("dve", lambda e, p_=p_, mcol=mcol: e.tensor_scalar(out=p_, in0=p_, scalar1=mcol, scalar2=None, op0=ALU.mult),
              [p_, mcol], [p_])
        elif d == 1:
            mcol = cst[:, C_MASK1 + i:C_MASK1 + i + 1]
            V("dve", lambda e, p_=p_, mcol=mcol: e.tensor_scalar(out=p_, in0=p_, scalar1=mcol, scalar2=None, op0=ALU.mult),
              [p_, mcol], [p_])
        w_ = wv[pi % 2][0:np_, :]
        V("dve", lambda e, w_=w_, vt_=vt_, p_=p_: e.tensor_tensor(out=w_.rearrange("p (h d) -> p h d", h=16),
                                                                  in0=vt_.rearrange("p (h d) -> p h d", h=16),
                                                                  in1=p_.unsqueeze(2).broadcast_to([p_.shape[0], 16, 64]),
                                                                  op=ALU.mult),
          [vt_, p_], [w_])
        sel = cst[0:np_, C_SEL + 4 * i:C_SEL + 4 * i + 4]
        for hh in range(2):
            V("pe", lambda e, hh=hh, sel=sel, w_=w_, first=first, last=last: e.matmul(num_ps[hh], lhsT=sel,
                                                                                       rhs=w_[:, hh * 512:(hh + 1) * 512],
                                                                                       start=first, stop=last),
              [sel, w_[:, hh * 512:(hh + 1) * 512]], [num_ps[hh]])
        V("pe", lambda e, sel=sel, p_=p_, first=first, last=last: e.matmul(den_ps, lhsT=sel, rhs=p_, start=first, stop=last),
          [sel, p_], [den_ps])

    V("dve", lambda e: e.reciprocal(out=recs[0:4, :], in_=den_ps), [den_ps], [recs[0:4, :]])
    for hh in range(2):
        o = mixs[0:4, hh * 512:(hh + 1) * 512].rearrange("p (h d) -> p h d", h=8)
        rr = recs[0:4, hh * 8:(hh + 1) * 8]
        V("dve", lambda e, o=o, hh=hh, rr=rr: e.tensor_tensor(out=o, in0=num_ps[hh].rearrange("p (h d) -> p h d", h=8),
                                                              in1=rr.unsqueeze(2).broadcast_to([4, 8, 64]), op=ALU.mult),
          [num_ps[hh], rr], [o])

    def gate_rows(z, dst):
        V("act", lambda e: e.activation(out=tmp4[0:4, :], in_=z, func=AF.Tanh, scale=0.5), [z], [tmp4[0:4, :]])
        V("dve", lambda e: e.scalar_tensor_tensor(out=tmp4[0:4, :], in0=tmp4[0:4, :], scalar=1.0, in1=z, op0=ALU.add, op1=ALU.mult),
          [tmp4[0:4, :], z], [tmp4[0:4, :]])
        V("dve", lambda e: e.scalar_tensor_tensor(out=dst, in0=dst, scalar=0.5, in1=tmp4[0:4, :], op0=ALU.mult, op1=ALU.mult),
          [dst, tmp4[0:4, :]], [dst])

    gate_rows(zga[0:4, :], mixs[0:4, 0:1024])
    pls_ps = PS(3)[:, 0:32].rearrange("p (c t) -> p c t", c=8)
    for ch in range(8):
        g = ch // 2
        pm = cst[0:19, C_PM + 4 * g:C_PM + 4 * g + 4]
        V("pe", lambda e, ch=ch, pm=pm: e.matmul(pls_ps[:, ch, :], lhsT=uext[0:19, ch * 128:(ch + 1) * 128], rhs=pm,
                                                 start=True, stop=True),
          [uext[0:19, ch * 128:(ch + 1) * 128], pm], [pls_ps[:, ch, :]])
    V("dve", lambda e: e.tensor_copy(out=plS, in_=pls_ps), [pls_ps], [plS])
    po_ps = PS(4)[0:4, :]
    po_ps2 = PS(5)[0:4, :]
    for g in range(4):
        dst = (po_ps if g < 2 else po_ps2)[:, (g % 2) * 256:(g % 2 + 1) * 256]
        for cc in range(2):
            V("pe", lambda e, g=g, cc=cc, dst=dst: e.matmul(dst, lhsT=plS[:, 2 * g + cc, :], rhs=wpb[:, g, cc, :],
                                                            start=(cc == 0), stop=(cc == 1)),
              [plS[:, 2 * g + cc, :], wpb[:, g, cc, :]], [dst])
    for half, pp_ in enumerate((po_ps, po_ps2)):
        dst = mixs[0:4, 1024 + half * 512:1024 + (half + 1) * 512]
        V("dve", lambda e, dst=dst, pp_=pp_, half=half: e.tensor_tensor(out=dst, in0=pp_[:, 0:512],
                                                                        in1=psr[0:4, half * 512:(half + 1) * 512], op=ALU.mult),
          [pp_[:, 0:512], psr[0:4, half * 512:(half + 1) * 512]], [dst])
    gate_rows(zgp[0:4, :], mixs[0:4, 1024:2048])
    V("dve", lambda e: e.tensor_copy(out=mixsb[0:4, :], in_=mixs[0:4, :]), [mixs[0:4, :]], [mixsb[0:4, :]])
    pstm = PS(6, BF16)[:, 0:64].rearrange("p (c t) -> p c t", c=16)
    for c in range(16):
        V("pe", lambda e, c=c: e.transpose(out=pstm[:, c, :], in_=mixsb[0:4, c * 128:(c + 1) * 128], identity=ident[0:4, 0:4]),
          [mixsb[0:4, c * 128:(c + 1) * 128], ident[0:4, 0:4]], [pstm[:, c, :]])
    V("dve", lambda e: e.tensor_copy(out=mixTs, in_=pstm), [pstm], [mixTs])

    if not want("E"):
        kb.finish()
        return nc
    be = Bump(R0, R1)
    mixT = be(16 * NOWN * 2).rearrange("p (e t) -> p e t", e=16)
    xtE = [be(D * 4, F32), be(D * 4, F32)]
    gBE = be(D * 4, F32)
    junkE = be(D * 2)
    ystE2 = [be(D * 4, F32), be(D * 4, F32)]
    for half in range(2):
        for e_ in range(16):
            kb.dma("sp", mixT[:, e_, half * 512:(half + 1) * 512], MIXT[e_][:, half * 512:(half + 1) * 512], "mixT%d" % half)
    kb.dma("sp", gBE, fg.broadcast_to([128, D]), "gBE")

    def out_rows(lhs_of_e, np_, xtile, col, ssb, rsb, ydst, yout, slot, bank0):
        for n4 in range(4):
            pb = PS(bank0 + n4)[0:np_, :]
            for e_ in range(16):
                V("pe", lambda e, e_=e_, n4=n4, pb=pb: e.matmul(pb, lhsT=lhs_of_e(e_), rhs=Wo[:, e_, n4 * 512:(n4 + 1) * 512],
                                                                 start=(e_ == 0), stop=(e_ == 15)),
                  [lhs_of_e(e_), Wo[:, e_, n4 * 512:(n4 + 1) * 512]], [pb])
            xsl = xtile[:, n4 * 512:(n4 + 1) * 512]
            V("dve", lambda e, pb=pb, xsl=xsl: e.tensor_tensor(out=xsl, in0=pb, in1=xsl, op=ALU.add), [pb, xsl], [xsl])
        norm_tile(None, xtile, np_, col, ssb, rsb, gBE, ydst, "e", junk_=junkE)
        kb.dma("sp", yout, ydst, slot)

    def x_load(tt):
        kb.dma("sp", xtE[tt % 2], xall[NHALO + tt * 128:NHALO + (tt + 1) * 128, :], "xtE%d" % (tt % 2))

    x_load(0)
    out_rows(lambda e_: mixTs[:, e_, :], 4, xs_sb[0:4, :], 9, ss2, rs2, ystE2[1][0:4, :], y_s, "ystE1", 4)
    for tt in range(8):
        xtile = xtE[tt % 2]
        if tt + 1 < 8:
            x_load(tt + 1)
        out_rows(lambda e_, tt=tt: mixT[:, e_, tt * 128:(tt + 1) * 128], 128, xtile, tt, ss2, rs2, ystE2[tt % 2],
                 y_own[tt * 128:(tt + 1) * 128, :], "ystE%d" % (tt % 2), 4 * (tt % 2))

    kb.finish()
    return nc


_NC_CACHE = {}


def _consts_for(chunk):
    c = np.zeros((128, NCONST), np.float32)
    k = np.arange(128)[:, None]
    col = np.arange(256)[None, :]
    M = ((col - k >= 0) & (col - k <= 128)).astype(np.float32)
    v0 = M.copy()
    v1 = M.copy()
    if chunk == 0:
        v1[:, 128:] = 0.0
    v2 = M.copy()
    kmin = {0: 128, 1: 64}.get(chunk, 0)
    v2[:kmin, 128:] = 0.0
    c[:, C_MASK:C_MASK + 256] = v0
    c[:, C_MASK + 256:C_MASK + 512] = v1
    c[:, C_MASK + 512:C_MASK + 768] = v2
    c[:, C_IDENT:C_IDENT + 128] = np.eye(128, dtype=np.float32)
    for i in range(4):
        c[:, C_SEL + 4 * i + i] = 1.0
        c[:, C_MASK1 + i] = (np.arange(128) >= i).astype(np.float32)
        for ip in range(4):
            c[ip, C_MNEW + i] = 3.0 if ip == i else (1.0 if ip < i else 0.0)
    s = 1024 * chunk
    for g in range(4):
        w = 2 ** (g + 1)
        for i in range(16):
            c[:, C_INVC + g * 16 + i] = 1.0 / min(s + i + 1, w)
        for i in range(4):
            for row in range(19):
                val = 0.0
                if 15 + i - w < row <= 15 + i:
                    val += 1.0 / w
                if row == 15 + i:
                    val -= 1.0
                c[row, C_PM + 4 * g + i] = val
    return c


def kernel(x_prompt, x_sample, cache_k, cache_v, state_pool, norm_g, w_in, w_pool, pool_scale, w_out,
           final_norm_g):
    f = np.float32
    x_prompt = np.asarray(x_prompt, f)
    x_sample = np.asarray(x_sample, f)
    cache_k = np.asarray(cache_k, f)
    cache_v = np.asarray(cache_v, f)
    state_pool = np.asarray(state_pool, f)
    if "nc" not in _NC_CACHE:
        _NC_CACHE["nc"] = build_nc()
    nc = _NC_CACHE["nc"]
    win = np.ascontiguousarray(np.asarray(w_in, f)[0])
    wpool = np.ascontiguousarray(np.asarray(w_pool, f)[0])
    wout = np.ascontiguousarray(np.asarray(w_out, f)[0])
    ps = np.asarray(pool_scale, f)[0]
    psfm = np.ascontiguousarray(ps.reshape(8, 128).T)
    psrow = np.ascontiguousarray(ps.reshape(1, 1024))
    ng = np.ascontiguousarray(np.asarray(norm_g, f)[0].reshape(1, D))
    fgv = np.ascontiguousarray(np.asarray(final_norm_g, f).reshape(1, D))
    in_maps = []
    for core in range(8):
        b, c = core // 4, core % 4
        s = 1024 * c
        xall = np.zeros((NT, D), f)
        lo = s - NHALO
        src_lo = max(lo, 0)
        xall[src_lo - lo:, :] = x_prompt[b, src_lo:s + NOWN, :]
        in_maps.append({
            "xall": xall,
            "xs": np.ascontiguousarray(x_sample[core]),
            "ck": np.ascontiguousarray(cache_k[0, core].reshape(2048, 1024)),
            "cv": np.ascontiguousarray(cache_v[0, core].reshape(2048, 1024)),
            "spool": np.ascontiguousarray(state_pool[0, core]),
            "ng": ng, "fg": fgv, "win": win, "wpool": wpool, "psfm": psfm, "psrow": psrow, "wout": wout,
            "consts": _consts_for(c),
        })
    res = run_bass_kernel_spmd(nc, in_maps, core_ids=list(range(8)))
    R = res.results
    y_prompt = np.zeros((2, 4096, D), f)
    k_prompt = np.zeros((1, 2, 2048, 16, 64), f)
    v_prompt = np.zeros((1, 2, 2048, 16, 64), f)
    pool_prompt = np.zeros((1, 2, 15, 1024), f)
    y_sample = np.zeros((8, 4, D), f)
    k_sample = np.zeros((1, 8, 2048, 16, 64), f)
    v_sample = np.zeros((1, 8, 2048, 16, 64), f)
    pool_sample = np.zeros((1, 8, 15, 1024), f)
    for core in range(8):
        b, c = core // 4, core % 4
        r = R[core]
        y_prompt[b, 1024 * c:1024 * (c + 1)] = r["y_own"]
        if c >= 2:
            k_prompt[0, b, 1024 * (c - 2):1024 * (c - 1)] = r["k_own"].reshape(1024, 16, 64)
            v_prompt[0, b, 1024 * (c - 2):1024 * (c - 1)] = r["v_own"].reshape(1024, 16, 64)
        if c == 3:
            pool_prompt[0, b] = r["pool_p"][1:16]
        y_sample[core] = r["y_s"]
        k_sample[0, core] = r["k_samp"].reshape(2048, 16, 64)
        v_sample[0, core] = r["v_samp"].reshape(2048, 16, 64)
        pool_sample[0, core] = r["pool_s"]
    return (y_prompt, y_sample, k_prompt, v_prompt, pool_prompt, k_sample, v_sample, pool_sample)
```
